# Optimizing a Trainium2 kernel written in Bass

```python
import jax, jax.numpy as jnp
from jax import lax
import numpy as np

D_MODEL = 2048
BATCH = 8
SEQ = 2048
DEPTH = 1

GRID_W = 64
MEM_LEN = 256
Q_BLOCK = 128
ROPE_THETA = 10000.0

A_HEADS = 8
A_KV_HEADS = 2
A_GROUP = A_HEADS // A_KV_HEADS
A_HEAD_DIM = 128
A_WIDTH = A_HEADS * A_HEAD_DIM

B_HEADS = 8
B_Q_RANK = 512
B_KV_RANK = 512
B_NOPE = 128
B_ROPE = 64
B_V = 128
B_WIDTH = B_HEADS * B_V

MIX_WIDTH = A_WIDTH + B_WIDTH

COLS_QA = A_HEADS * A_HEAD_DIM
COLS_KA = A_KV_HEADS * A_HEAD_DIM
COLS_VA = A_KV_HEADS * A_HEAD_DIM
COLS_CQ = B_Q_RANK
COLS_CKV = B_KV_RANK
COLS_KR = B_ROPE
IN_COLS = COLS_QA + COLS_KA + COLS_VA + COLS_CQ + COLS_CKV + COLS_KR
SPLITS = (COLS_QA,
          COLS_QA + COLS_KA,
          COLS_QA + COLS_KA + COLS_VA,
          COLS_QA + COLS_KA + COLS_VA + COLS_CQ,
          COLS_QA + COLS_KA + COLS_VA + COLS_CQ + COLS_CKV)

MEM_HEADS = 4
MEM_HEAD_DIM = 128
MEM_WIDTH = MEM_HEADS * MEM_HEAD_DIM

D_FF = 4 * D_MODEL

ALPHA = (2.0 * DEPTH) ** 0.25
BETA = (8.0 * DEPTH) ** -0.25

LN_EPS = 1e-5
RMS_EPS = 1e-6

kernel_name = "hybrid_gqa_mla_deepnorm_encoder"


def layer_norm(x, g, b):
    xf = x.astype(jnp.float32)
    mu = jnp.mean(xf, axis=-1, keepdims=True)
    var = jnp.mean(jnp.square(xf - mu), axis=-1, keepdims=True)
    y = (xf - mu) * lax.rsqrt(var + LN_EPS) * g.astype(jnp.float32) + b.astype(jnp.float32)
    return y.astype(x.dtype)


def rms_norm(x, g):
    xf = x.astype(jnp.float32)
    y = xf * lax.rsqrt(jnp.mean(jnp.square(xf), axis=-1, keepdims=True) + RMS_EPS) * g.astype(jnp.float32)
    return y.astype(x.dtype)


def axial_rope_angles(seq_len, rot_dim):
    rows = seq_len // GRID_W
    row = jnp.repeat(jnp.arange(rows, dtype=jnp.float32), GRID_W)
    col = jnp.tile(jnp.arange(GRID_W, dtype=jnp.float32), rows)
    quarter = rot_dim // 4
    inv_freq = ROPE_THETA ** (-jnp.arange(quarter, dtype=jnp.float32) / quarter)
    return jnp.concatenate([row[:, None] * inv_freq, col[:, None] * inv_freq], axis=-1)


def apply_rope(x, ang):
    xf = x.astype(jnp.float32).reshape(*x.shape[:-1], x.shape[-1] // 2, 2)
    cos = jnp.cos(ang)[None, :, None, :]
    sin = jnp.sin(ang)[None, :, None, :]
    x1, x2 = xf[..., 0], xf[..., 1]
    out = jnp.stack([x1 * cos - x2 * sin, x1 * sin + x2 * cos], axis=-1).reshape(x.shape)
    return out.astype(x.dtype)


def block_attention(q, k, v):
    b, s, kvh, g, dk = q.shape
    dv = v.shape[-1]
    nb = s // Q_BLOCK
    scale = dk ** -0.5
    qb = q.reshape(b, nb, Q_BLOCK, kvh, g, dk).transpose(1, 0, 2, 3, 4, 5)

    def one_block(q_blk):
        scores = jnp.einsum('bqkgd,bskd->bkgqs', q_blk, k,
                            preferred_element_type=jnp.float32) * scale
        probs = jax.nn.softmax(scores, axis=-1).astype(v.dtype)
        return jnp.einsum('bkgqs,bskd->bqkgd', probs, v)

    out = lax.map(one_block, qb)
    return out.transpose(1, 0, 2, 3, 4, 5).reshape(b, s, kvh * g * dv)


def hybrid_mixer(h, ang_a, ang_b, w_in, g_qa, g_ka, g_cq, w_uq, g_ckv, w_ukv, w_out):
    b, s, _ = h.shape
    proj = h @ w_in
    q_a, k_a, v_a, c_q, c_kv, k_r = jnp.split(proj, SPLITS, axis=-1)

    q_a = apply_rope(rms_norm(q_a.reshape(b, s, A_HEADS, A_HEAD_DIM), g_qa), ang_a)
    k_a = apply_rope(rms_norm(k_a.reshape(b, s, A_KV_HEADS, A_HEAD_DIM), g_ka), ang_a)
    v_a = v_a.reshape(b, s, A_KV_HEADS, A_HEAD_DIM)
    o_a = block_attention(q_a.reshape(b, s, A_KV_HEADS, A_GROUP, A_HEAD_DIM), k_a, v_a)

    q_b = (rms_norm(c_q, g_cq) @ w_uq).reshape(b, s, B_HEADS, B_NOPE + B_ROPE)
    q_b = jnp.concatenate([q_b[..., :B_NOPE], apply_rope(q_b[..., B_NOPE:], ang_b)], axis=-1)
    kv_b = (rms_norm(c_kv, g_ckv) @ w_ukv).reshape(b, s, B_HEADS, B_NOPE + B_V)
    k_nope, v_b = kv_b[..., :B_NOPE], kv_b[..., B_NOPE:]
    k_rope = apply_rope(k_r.reshape(b, s, 1, B_ROPE), ang_b)
    k_b = jnp.concatenate([k_nope, jnp.broadcast_to(k_rope, (b, s, B_HEADS, B_ROPE))], axis=-1)
    o_b = block_attention(q_b[:, :, :, None, :], k_b, v_b)

    return jnp.concatenate([o_a, o_b], axis=-1) @ w_out


def memory_cross_attention(h, mem_n, w_q_mem, w_kv_mem, w_o_mem):
    b, s, _ = h.shape
    m = mem_n.shape[1]
    q = (h @ w_q_mem).reshape(b, s, MEM_HEADS, MEM_HEAD_DIM)
    kv = (mem_n @ w_kv_mem).reshape(b, m, 2, MEM_HEADS, MEM_HEAD_DIM)
    k, v = kv[:, :, 0], kv[:, :, 1]
    scores = jnp.einsum('bqhd,bmhd->bhqm', q, k,
                        preferred_element_type=jnp.float32) * (MEM_HEAD_DIM ** -0.5)
    probs = jax.nn.softmax(scores, axis=-1).astype(v.dtype)
    o = jnp.einsum('bhqm,bmhd->bqhd', probs, v).reshape(b, s, MEM_WIDTH)
    return o @ w_o_mem


def squared_relu_mlp(h, w_mlp_in, w_mlp_out):
    return jnp.square(jax.nn.relu(h @ w_mlp_in)) @ w_mlp_out


def setup_inputs(seed: int = 0) -> dict:
    key = jax.random.key(seed)
    ks = jax.random.split(key, 24)
    f32 = jnp.float32

    def w(k, fan_in, fan_out, scale=1.0):
        return jax.random.normal(k, (DEPTH, fan_in, fan_out), f32) * (fan_in ** -0.5) * scale

    def gain(k, n):
        return 1.0 + 0.02 * jax.random.normal(k, (DEPTH, n), f32)

    def bias(k, n):
        return 0.02 * jax.random.normal(k, (DEPTH, n), f32)

    return {
        "x": jax.random.normal(ks[0], (BATCH, SEQ, D_MODEL), f32),
        "mem": jax.random.normal(ks[1], (BATCH, MEM_LEN, D_MODEL), f32),
        "w_in": w(ks[2], D_MODEL, IN_COLS),
        "g_qa": gain(ks[3], A_HEAD_DIM),
        "g_ka": gain(ks[4], A_HEAD_DIM),
        "g_cq": gain(ks[5], B_Q_RANK),
        "w_uq": w(ks[6], B_Q_RANK, B_HEADS * (B_NOPE + B_ROPE)),
        "g_ckv": gain(ks[7], B_KV_RANK),
        "w_ukv": w(ks[8], B_KV_RANK, B_HEADS * (B_NOPE + B_V)),
        "w_out": w(ks[9], MIX_WIDTH, D_MODEL, BETA),
        "ln1_g": gain(ks[10], D_MODEL),
        "ln1_b": bias(ks[11], D_MODEL),
        "mem_ln_g": gain(ks[12], D_MODEL),
        "mem_ln_b": bias(ks[13], D_MODEL),
        "w_q_mem": w(ks[14], D_MODEL, MEM_WIDTH),
        "w_kv_mem": w(ks[15], D_MODEL, 2 * MEM_WIDTH),
        "w_o_mem": w(ks[16], MEM_WIDTH, D_MODEL, BETA),
        "ln2_g": gain(ks[17], D_MODEL),
        "ln2_b": bias(ks[18], D_MODEL),
        "w_mlp_in": w(ks[19], D_MODEL, D_FF, BETA),
        "w_mlp_out": w(ks[20], D_FF, D_MODEL, BETA),
        "ln3_g": gain(ks[21], D_MODEL),
        "ln3_b": bias(ks[22], D_MODEL),
    }


def reference(x, mem, w_in, g_qa, g_ka, g_cq, w_uq, g_ckv, w_ukv, w_out,
              ln1_g, ln1_b, mem_ln_g, mem_ln_b, w_q_mem, w_kv_mem, w_o_mem,
              ln2_g, ln2_b, w_mlp_in, w_mlp_out, ln3_g, ln3_b):
    seq_len = x.shape[1]
    ang_a = axial_rope_angles(seq_len, A_HEAD_DIM)
    ang_b = axial_rope_angles(seq_len, B_ROPE)

    for l in range(DEPTH):
        mix = hybrid_mixer(x, ang_a, ang_b, w_in[l], g_qa[l], g_ka[l], g_cq[l], w_uq[l],
                           g_ckv[l], w_ukv[l], w_out[l])
        x = layer_norm(ALPHA * x + mix, ln1_g[l], ln1_b[l])

        mem_n = layer_norm(mem, mem_ln_g[l], mem_ln_b[l])
        xa = memory_cross_attention(x, mem_n, w_q_mem[l], w_kv_mem[l], w_o_mem[l])
        x = layer_norm(ALPHA * x + xa, ln2_g[l], ln2_b[l])

        ff = squared_relu_mlp(x, w_mlp_in[l], w_mlp_out[l])
        x = layer_norm(ALPHA * x + ff, ln3_g[l], ln3_b[l])
    return x
```

```python
import os
import types
import numpy as np
from contextlib import ExitStack
import concourse.bass as bass
import concourse.mybir as mybir
from concourse.bass_utils import run_bass_kernel_spmd

F32 = mybir.dt.float32
BF16 = mybir.dt.bfloat16
AF = mybir.ActivationFunctionType
ALU = mybir.AluOpType
AX = mybir.AxisListType

S = 2048
D = 2048
MEM = 256
NCORES = 8
IN_COLS = 2624
DFF = 8192
ALPHA = 2.0 ** 0.25
LN_EPS = 1e-5
RMS_EPS = 1e-6
SC_A = 128.0 ** -0.5
SC_B = 192.0 ** -0.5
SC_M = 128.0 ** -0.5

DEBUG_OUT = False
LAST_PHASE = "E"
DBG = 99


def _freeze(fn):
    if getattr(fn, "__closure__", None) is None:
        return fn
    cells = []
    for c in fn.__closure__:
        try:
            cells.append(types.CellType(c.cell_contents))
        except ValueError:
            cells.append(c)
    g = types.FunctionType(fn.__code__, fn.__globals__, fn.__name__, fn.__defaults__, tuple(cells))
    g.__kwdefaults__ = fn.__kwdefaults__
    return g


class Prog:
    ENG = {"pe": "tensor", "act": "scalar", "dve": "vector", "pool": "gpsimd", "sp": "sync"}

    def __init__(self, nc, es):
        self.nc = nc
        self.es = es
        self.ops = {e: [] for e in self.ENG}
        self.sems = {}
        self.cnt = {}
        self.unit = {}
        self.clock = {e: {} for e in self.ENG}
        self.evclock = {}
        self.lastw = {}
        self.readers = {}
        for e in self.ENG:
            self.source(e, 1)

    def source(self, name, unit=16):
        if name not in self.sems:
            self.sems[name] = self.es.enter_context(self.nc.semaphore("s_" + name))
            self.cnt[name] = 0
            self.unit[name] = unit
        return name

    def _need(self, eng, ev, waits):
        if ev is None:
            return
        src, c = ev
        if src == eng == "pe":
            return
        ck = self.clock[eng]
        if ck.get(src, 0) >= c:
            return
        waits[src] = max(waits.get(src, 0), c)
        for s, v in self.evclock[ev].items():
            if ck.get(s, 0) < v:
                ck[s] = v

    mute = False

    def emit(self, eng, fn, reads=(), writes=(), src=None):
        if self.mute:
            return None
        waits = {}
        writes = list(writes) + [b for b in reads if b.startswith("ps")]
        reads = [b for b in reads if not b.startswith("ps")]
        for b in reads:
            self._need(eng, self.lastw.get(b), waits)
        for b in writes:
            self._need(eng, self.lastw.get(b), waits)
            for ev in self.readers.get(b, ()):
                self._need(eng, ev, waits)
        if src is None:
            src = eng
        self.cnt[src] += 1
        ev = (src, self.cnt[src])
        ck = dict(self.clock[eng])
        ck[src] = self.cnt[src]
        self.evclock[ev] = ck
        for b in reads:
            self.readers.setdefault(b, []).append(ev)
        for b in writes:
            self.lastw[b] = ev
            self.readers[b] = []
        self.ops[eng].append((sorted(waits.items()), _freeze(fn), src))
        if os.environ.get("DBGTRACE"):
            print("EMIT", eng, ev, "waits", sorted(waits.items()), "R", list(reads), "W", list(writes))
        return ev

    def dma(self, q, out, in_, reads=(), writes=(), src=None):
        self.source(src, 16)
        return self.emit(q, lambda e: e.dma_start(out=out, in_=in_), reads=reads, writes=writes, src=src)

    def dma_batch(self, q, items, src, reads=()):
        self.source(src, 16)
        keys = [k for (_, _, k) in items]
        ev = None
        for n, (o, i, k) in enumerate(items):
            ev = self.emit(q, lambda e, o=o, i=i: e.dma_start(out=o, in_=i), reads=reads if n == 0 else (),
                           writes=keys if n == 0 else (), src=src)
        if ev is None:
            return None
        for k in keys:
            self.lastw[k] = ev
            self.readers[k] = []
        return ev

    def fence(self):
        for e in self.ENG:
            waits = []
            for s, c in self.cnt.items():
                if c > 0 and self.clock[e].get(s, 0) < c:
                    waits.append((s, c))
                    self.clock[e][s] = c
            self.ops[e].append((sorted(waits), None, None))
        self.lastw = {}
        self.readers = {}
        self.evclock = {}

    def build(self):
        with self.nc.Block() as block:
            for e, hname in self.ENG.items():
                ops = self.ops[e]

                def body(eng, ops=ops):
                    for waits, fn, src in ops:
                        for s, c in waits:
                            eng.wait_ge(self.sems[s], c * self.unit[s])
                        if fn is not None:
                            fn(eng).then_inc(self.sems[src], self.unit[src])

                getattr(block, hname)(body)
        self.ops = {e: [] for e in self.ENG}


def build_program():
    nc = bass.Bass("TRN2", target_bir_lowering=False)

    def din(name, shape, dt=F32):
        return nc.dram_tensor(name, shape, dt, kind="ExternalInput").ap()

    def dscr(name, shape, dt):
        kind = "ExternalOutput" if DEBUG_OUT else "Internal"
        return nc.dram_tensor(name, shape, dt, kind=kind).ap()

    x = din("x", [S, D])
    mem = din("mem", [MEM, D])
    w_in = din("w_in", [D, IN_COLS])
    g_qa = din("g_qa", [1, 128]); g_ka = din("g_ka", [1, 128])
    g_cq = din("g_cq", [1, 512]); g_ckv = din("g_ckv", [1, 512])
    w_uq = din("w_uq", [512, 1536]); w_ukv = din("w_ukv", [512, 2048])
    w_out = din("w_out", [D, D])
    ln1_g = din("ln1_g", [1, D]); ln1_b = din("ln1_b", [1, D])
    mem_ln_g = din("mem_ln_g", [1, D]); mem_ln_b = din("mem_ln_b", [1, D])
    w_q_mem = din("w_q_mem", [D, 512]); w_kv_mem = din("w_kv_mem", [D, 1024]); w_o_mem = din("w_o_mem", [512, D])
    ln2_g = din("ln2_g", [1, D]); ln2_b = din("ln2_b", [1, D])
    w_mlp_in = din("w_mlp_in", [D, DFF]); w_mlp_out = din("w_mlp_out", [DFF, D])
    ln3_g = din("ln3_g", [1, D]); ln3_b = din("ln3_b", [1, D])
    cosA = din("cosA", [S, 64]); sinA = din("sinA", [S, 64])
    cosB = din("cosB", [S, 32]); sinB = din("sinB", [S, 32])
    ident_in = din("ident", [128, 128])
    out = nc.dram_tensor("out", [S, D], F32, kind="ExternalOutput").ap()

    QA = dscr("QA", [8, 128, S], BF16); KA = dscr("KA", [2, 128, S], BF16); VA = dscr("VA", [S, 256], BF16)
    QBN = dscr("QBN", [8, 128, S], BF16); QBR = dscr("QBR", [4, 128, S], BF16)
    KBN = dscr("KBN", [8, 128, S], BF16); KR = dscr("KR", [128, S], BF16); VB = dscr("VB", [S, 1024], BF16)
    OT = dscr("OT", [16, 128, S], BF16)
    X1 = dscr("X1", [S, D], F32); X2 = dscr("X2", [S, D], F32)
    W1s = nc.dram_tensor("W1s", [32, 128, 16, 256], BF16).ap()
    W2s = nc.dram_tensor("W2s", [2, 16, 128, 4, 1024], BF16).ap()

    with ExitStack() as ges:
        P = Prog(nc, ges)

        def gsb(name, shape, dt):
            return ges.enter_context(nc.sbuf_tensor("sb_" + name, shape, dt))

        ident = gsb("ident", [128, 128], BF16)
        ones = gsb("ones", [128, 128], BF16)
        eps_ln = gsb("eps_ln", [128, 1], F32)
        eps_rms = gsb("eps_rms", [128, 1], F32)
        kmT = gsb("kmT", [128, 4, 256], BF16)
        vm = gsb("vm", [128, 2, 512], BF16)
        psum = [ges.enter_context(nc.psum_tensor("psb%d" % i, [128, 512], F32)) for i in range(8)]

        def bank(i):
            return psum[i][:, :]

        def bankbf(i):
            return bank(i).bitcast(BF16)

        def pk(i):
            return "ps%d" % i

        bank_rr = [0]

        def next_bank():
            b = bank_rr[0]
            bank_rr[0] = (b + 1) % 8
            return b

        def layernorm(xp, xkey, gt, bt, pfx, st6, mv, sd, rstd, nb):
            for i in range(4):
                P.emit("dve", lambda e, i=i: e.bn_stats(out=st6[:, i, :], in_=xp[:, i * 512:(i + 1) * 512]),
                       reads=[xkey], writes=[pfx + "st%d" % i])
            P.emit("dve", lambda e: e.bn_aggr(out=mv[:], in_=st6[:]),
                   reads=[pfx + "st%d" % i for i in range(4)], writes=[pfx + "mv"])
            P.emit("act", lambda e: e.activation(out=sd[:], in_=mv[:, 1:2], func=AF.Sqrt, bias=eps_ln[:, 0:1], scale=1.0),
                   reads=[pfx + "mv", "eps"], writes=[pfx + "sd"])
            P.emit("dve", lambda e: e.reciprocal(out=rstd[:], in_=sd[:]), reads=[pfx + "sd"], writes=[pfx + "rstd"])
            P.emit("dve", lambda e: e.scalar_tensor_tensor(out=nb[:], in0=mv[:, 0:1], scalar=-1.0, in1=rstd[:],
                                                           op0=ALU.mult, op1=ALU.mult),
                   reads=[pfx + "mv", pfx + "rstd"], writes=[pfx + "nb"])
            P.emit("act", lambda e: e.activation(out=xp, in_=xp, func=AF.Identity, scale=rstd[:, 0:1], bias=nb[:, 0:1]),
                   reads=[xkey, pfx + "rstd", pfx + "nb"], writes=[xkey])
            P.emit("pool", lambda e: e.tensor_tensor(out=xp, in0=xp, in1=gt[:], op=ALU.mult),
                   reads=[xkey, "lng"], writes=[xkey])
            P.emit("pool", lambda e: e.tensor_tensor(out=xp, in0=xp, in1=bt[:], op=ALU.add),
                   reads=[xkey, "lnb"], writes=[xkey])

        def transpose_tiles(src_tiles, src_keys, dsts, b=None):
            if b is None:
                b = next_bank()
            pb = bankbf(b)

            def fn(e):
                ins = None
                for i, t in enumerate(src_tiles):
                    n = t.shape[-1]
                    ins = e.transpose(out=pb[0:n, i * 128:(i + 1) * 128], in_=t, identity=ident[:])
                return ins
            P.emit("pe", fn, reads=list(src_keys) + ["ident"], writes=[pk(b)])
            for dst, dkey, first, cnt, rows, eng in dsts:
                srcv = pb[0:rows, first * 128:(first + cnt) * 128].rearrange("p (c t) -> p c t", t=128)
                if eng == "act":
                    P.emit("act", lambda e, d=dst, s=srcv: e.copy(out=d, in_=s), reads=[pk(b)], writes=[dkey])
                else:
                    P.emit("dve", lambda e, d=dst, s=srcv: e.tensor_copy(out=d, in_=s), reads=[pk(b)], writes=[dkey])

        with ExitStack() as es:
            idf = es.enter_context(nc.sbuf_tensor("idf", [128, 128], F32))
            P.dma("sp", idf[:], ident_in, writes=["idf"], src="d_misc")
            P.emit("dve", lambda e: e.tensor_copy(out=ident[:], in_=idf[:]), reads=["idf"], writes=["ident"])
            P.emit("dve", lambda e: e.memset(ones[:], 1.0), writes=["ones"])
            P.emit("dve", lambda e: e.memset(eps_ln[:], LN_EPS), writes=["eps"])
            P.emit("dve", lambda e: e.memset(eps_rms[:], RMS_EPS), writes=["eps"])
            P.fence()
            P.build()

        es_a = ExitStack()
        if LAST_PHASE >= "A":
            sba = lambda n, s, d: es_a.enter_context(nc.sbuf_tensor("sb_" + n, s, d))
            win_t = sba("win", [128, 16, IN_COLS], BF16)
            wuq_t = sba("wuq", [128, 4, 1536], BF16)
            wukv_t = sba("wukv", [128, 4, 2048], BF16)

        if not os.environ.get("SKIP0"):
          with ExitStack() as es:
              sb = lambda n, s, d: es.enter_context(nc.sbuf_tensor("sb_" + n, s, d))
              wkv = sb("wkv", [128, 16, 1024], BF16)
              gt = sb("gt0", [128, D], F32); bt = sb("bt0", [128, D], F32)
              mt_ = [sb("memt%d" % i, [128, D], F32) for i in range(2)]
              mb = sb("memb", [128, D], BF16)
              memT = sb("memT", [128, 16, 256], BF16)
              st6 = sb("st6_0", [128, 4, 6], F32); mv = sb("mv_0", [128, 2], F32)
              sd = sb("sd_0", [128, 1], F32); rstd = sb("rstd_0", [128, 1], F32); nb = sb("nb_0", [128, 1], F32)
              wsrc = w_kv_mem.rearrange("(k p) c -> p k c", p=128)
              P.dma_batch("pool", [(wkv[:, 4 * i:4 * i + 4, :], wsrc[:, 4 * i:4 * i + 4, :], "wkv") for i in range(4)], "d_w0")
              if LAST_PHASE >= "A":
                  wsrc_i = w_in.rearrange("(k p) c -> p k c", p=128)
                  for k in range(16):
                      P.dma("pool", win_t[:, k, :], wsrc_i[:, k, :], src="d_pa")
                  P.dma("pool", wuq_t[:], w_uq.rearrange("(k p) c -> p k c", p=128), src="d_pa")
                  P.dma("pool", wukv_t[:], w_ukv.rearrange("(k p) c -> p k c", p=128), src="d_pa")
              P.dma("sp", gt[:], mem_ln_g.partition_broadcast(128), writes=["lng"], src="d_g")
              P.dma("sp", bt[:], mem_ln_b.partition_broadcast(128), writes=["lnb"], src="d_b")
              for m in range(2):
                  P.dma("sp", mt_[m][:], mem[m * 128:(m + 1) * 128, :], writes=["memt%d" % m], src="d_x%d" % m)
              for m in range(2):
                  layernorm(mt_[m][:], "memt%d" % m, gt, bt, "l0", st6, mv, sd, rstd, nb)
                  P.emit("dve", lambda e, m=m: e.tensor_copy(out=mb[:], in_=mt_[m][:]), reads=["memt%d" % m], writes=["memb"])
                  for half in range(2):
                      transpose_tiles([mb[:, (half * 8 + i) * 128:(half * 8 + i + 1) * 128] for i in range(8)], ["memb"],
                                      [(memT[:, half * 8:half * 8 + 8, m * 128:(m + 1) * 128], "memT", 0, 8, 128, "act")])
              for h in range(4):
                  b = next_bank()

                  def fn(e, h=h, b=b):
                      ins = None
                      for k in range(16):
                          ins = e.matmul(bank(b)[:, 0:256], lhsT=wkv[:, k, h * 128:(h + 1) * 128], rhs=memT[:, k, :],
                                         start=(k == 0), stop=(k == 15))
                      return ins
                  P.emit("pe", fn, reads=["wkv", "memT"], writes=[pk(b)])
                  P.emit("act", lambda e, h=h, b=b: e.copy(out=kmT[:, h, :], in_=bank(b)[:, 0:256]), reads=[pk(b)], writes=["kmT"])
              for m in range(2):
                  b = next_bank()

                  def fn(e, m=m, b=b):
                      ins = None
                      for k in range(16):
                          ins = e.matmul(bank(b), lhsT=memT[:, k, m * 128:(m + 1) * 128], rhs=wkv[:, k, 512:1024],
                                         start=(k == 0), stop=(k == 15))
                      return ins
                  P.emit("pe", fn, reads=["wkv", "memT"], writes=[pk(b)])
                  P.emit("dve", lambda e, m=m, b=b: e.tensor_copy(out=vm[:, m, :], in_=bank(b)), reads=[pk(b)], writes=["vm"])
              P.fence()
              P.build()

        if LAST_PHASE >= "A":
          with ExitStack() as es:
            sb = lambda n, s, d: es.enter_context(nc.sbuf_tensor("sb_" + n, s, d))
            win = win_t
            wuq = wuq_t
            wukv = wukv_t
            gqa = sb("gqa", [128, 128], F32); gka = sb("gka", [128, 128], F32)
            gcq = sb("gcq", [128, 512], F32); gckv = sb("gckv", [128, 512], F32)
            tcA = sb("tcA", [128, 4, 64], F32); tsA = sb("tsA", [128, 4, 64], F32)
            tcB = sb("tcB", [128, 4, 32], F32); tsB = sb("tsB", [128, 4, 32], F32)
            xt = [sb("xtA%d" % i, [128, D], F32) for i in range(2)]
            xb = sb("xbA", [128, D], BF16)
            xT = sb("xTA", [128, 16, 128], BF16)
            ssq = sb("ssq", [128, 12], F32); sdq = sb("sdq", [128, 12], F32); rsq = sb("rsq", [128, 12], F32)
            sqj = sb("sqj", [128, 512], F32)
            xn = sb("xn", [128, 10, 128], F32)
            t1 = sb("t1", [128, 10, 64], F32); t2 = sb("t2", [128, 10, 64], F32)
            qkb = sb("qkb", [128, 10, 128], BF16)
            cqn = sb("cqn", [128, 512], BF16); ckvn = sb("ckvn", [128, 512], BF16)
            krb = sb("krb", [128, 2, 32, 2], BF16)
            u1 = sb("u1", [128, 32], F32); u2 = sb("u2", [128, 32], F32)
            vab = sb("vab", [128, 256], BF16)
            qaT = sb("qaT", [128, 8, 512], BF16); kaT = sb("kaT", [128, 2, 512], BF16)
            cqT = sb("cqT", [128, 4, 512], BF16); ckvT = sb("ckvT", [128, 4, 512], BF16)
            krT = sb("krT", [128, 512], BF16)
            pst = sb("pst", [128, IN_COLS], F32)
            nopeT = pst[:, 0:2048].bitcast(BF16).rearrange("p (h t) -> p h t", t=512)
            PSTK = ["pst%d" % i for i in range(6)]
            NOPEK = ["nopeT%d" % i for i in range(8)]
            barA = sb("barA", [128, 1], F32)
            qbrT = sb("qbrT", [128, 4, 512], BF16)
            qrb = sb("qrb", [128, 8, 32, 2], BF16)
            v1 = sb("v1", [128, 8, 32], F32); v2 = sb("v2", [128, 8, 32], F32)
            vbb = sb("vbb", [128, 1024], BF16)

            aux = os.environ.get("NOAUX", "")
            P.mute = "g" in aux
            P.dma_batch("sp", [(gqa[:], g_qa.partition_broadcast(128), "gqa"), (gka[:], g_ka.partition_broadcast(128), "gka"),
                               (gcq[:], g_cq.partition_broadcast(128), "gcq"), (gckv[:], g_ckv.partition_broadcast(128), "gckv")], "d_g")

            P.mute = False

            def load_x(tt):
                P.dma("sp", xt[tt % 2][:], x[tt * 128:(tt + 1) * 128, :], writes=["xt%d" % (tt % 2)], src="d_x%d" % (tt % 2))

            GROUPS = [(0, 512), (512, 1024), (1024, 1536), (1536, 2048), (2048, 2560), (2560, 2624)]
            trb = [0]

            def tr_bank():
                trb[0] ^= 1
                return 6 + trb[0]

            def front(c, j):
                tt = c * 4 + j
                xk = "xt%d" % (tt % 2)
                xtt = xt[tt % 2]
                if tt + 1 < 16:
                    load_x(tt + 1)
                P.emit("dve", lambda e: e.tensor_copy(out=xb[:], in_=xtt[:]), reads=[xk], writes=["xb"])
                for half in range(2):
                    transpose_tiles([xb[:, (half * 8 + i) * 128:(half * 8 + i + 1) * 128] for i in range(8)], ["xb"],
                                    [(xT[:, half * 8:half * 8 + 8, :], "xT", 0, 8, 128, "act" if half else "dve")], b=tr_bank())
                for gi, (lo, hi) in enumerate(GROUPS):
                    def fn(e, lo=lo, hi=hi, gi=gi):
                        ins = None
                        for k in range(16):
                            ins = e.matmul(bank(gi)[:, 0:hi - lo], lhsT=xT[:, k, :], rhs=win[:, k, lo:hi],
                                           start=(k == 0), stop=(k == 15))
                        return ins
                    P.emit("pe", fn, reads=["xT", "win"], writes=[pk(gi)])

            def evac(c, j):
                for gi, (lo, hi) in enumerate(GROUPS):
                    if gi % 2 == 0:
                        P.emit("act", lambda e, gi=gi, lo=lo, hi=hi: e.copy(out=pst[:, lo:hi], in_=bank(gi)[:, 0:hi - lo]),
                               reads=[pk(gi)], writes=[PSTK[gi]] + (NOPEK if j == 0 else []))
                    else:
                        P.emit("dve", lambda e, gi=gi, lo=lo, hi=hi: e.tensor_copy(out=pst[:, lo:hi], in_=bank(gi)[:, 0:hi - lo]),
                               reads=[pk(gi)], writes=[PSTK[gi]] + (NOPEK if j == 0 else []))

            def chain_a(c, j):
                tt = c * 4 + j
                P.emit("dve", lambda e: e.memset(ssq[:], 0.0), writes=["ssq%d" % i for i in range(12)])
                for hh in range(10):
                    P.emit("act", lambda e, hh=hh: e.activation(out=sqj[:, 0:128], in_=pst[:, hh * 128:(hh + 1) * 128],
                                                                func=AF.Square, accum_out=ssq[:, hh:hh + 1]),
                           reads=[PSTK[hh // 4]], writes=["sqj", "ssq%d" % hh])
                for i, b in ((10, 3), (11, 4)):
                    P.emit("act", lambda e, i=i, b=b: e.activation(out=sqj[:], in_=pst[:, 1536 + (b - 3) * 512:2048 + (b - 3) * 512],
                                                                   func=AF.Square, accum_out=ssq[:, i:i + 1]),
                           reads=[PSTK[b]], writes=["sqj", "ssq%d" % i])
                P.emit("pool", lambda e: e.tensor_copy(out=vab[:], in_=pst[:, 1280:1536]), reads=[PSTK[2]], writes=["vab"])
                P.dma("sp", VA[tt * 128:(tt + 1) * 128, :], vab[:], reads=["vab"], src="d_va")
                P.emit("act", lambda e: e.activation(out=sdq[:, 0:10], in_=ssq[:, 0:10], func=AF.Sqrt, bias=eps_rms[:, 0:1], scale=1.0 / 128),
                       reads=["ssq%d" % i for i in range(10)] + ["eps"], writes=["sdq_a"])
                P.emit("act", lambda e: e.activation(out=sdq[:, 10:12], in_=ssq[:, 10:12], func=AF.Sqrt, bias=eps_rms[:, 0:1], scale=1.0 / 512),
                       reads=["ssq10", "ssq11", "eps"], writes=["sdq_b"])
                P.emit("dve", lambda e: e.reciprocal(out=rsq[:], in_=sdq[:]), reads=["sdq_a", "sdq_b"], writes=["rsq"])
                P.emit("dve", lambda e: e.tensor_tensor(
                    out=xn[:], in0=pst[:, 0:1280].rearrange("p (h d) -> p h d", d=128),
                    in1=rsq[:, 0:10].unsqueeze(2).broadcast_to([128, 10, 128]), op=ALU.mult),
                    reads=[PSTK[0], PSTK[1], PSTK[2], "rsq"], writes=["xn"])
                P.emit("dve", lambda e: e.scalar_tensor_tensor(out=cqn[:], in0=pst[:, 1536:2048], scalar=rsq[:, 10:11], in1=gcq[:], op0=ALU.mult, op1=ALU.mult),
                       reads=[PSTK[3], "rsq", "gcq"], writes=["cqn"])
                P.emit("dve", lambda e: e.scalar_tensor_tensor(out=ckvn[:], in0=pst[:, 2048:2560], scalar=rsq[:, 11:12], in1=gckv[:], op0=ALU.mult, op1=ALU.mult),
                       reads=[PSTK[4], "rsq", "gckv"], writes=["ckvn"])
                kv5 = pst[:, 2560:2624].rearrange("p (i two) -> p i two", two=2)
                ke, ko = kv5[:, :, 0], kv5[:, :, 1]
                cB = tcB[:, j, :]
                sB = tsB[:, j, :]
                P.emit("dve", lambda e: e.tensor_tensor(out=u1[:], in0=ke, in1=cB, op=ALU.mult), reads=[PSTK[5], "tab"], writes=["u1"])
                P.emit("dve", lambda e: e.tensor_tensor(out=u2[:], in0=ko, in1=sB, op=ALU.mult), reads=[PSTK[5], "tab"], writes=["u2"])
                P.emit("dve", lambda e: e.tensor_tensor(out=krb[:, :, :, 0], in0=u1[:].unsqueeze(1).broadcast_to([128, 2, 32]),
                                                        in1=u2[:].unsqueeze(1).broadcast_to([128, 2, 32]), op=ALU.subtract),
                       reads=["u1", "u2"], writes=["krb"])
                P.emit("dve", lambda e: e.tensor_tensor(out=u1[:], in0=ke, in1=sB, op=ALU.mult), reads=[PSTK[5], "tab"], writes=["u1"])
                P.emit("dve", lambda e: e.tensor_tensor(out=u2[:], in0=ko, in1=cB, op=ALU.mult), reads=[PSTK[5], "tab"], writes=["u2"])
                P.emit("dve", lambda e: e.tensor_tensor(out=krb[:, :, :, 1], in0=u1[:].unsqueeze(1).broadcast_to([128, 2, 32]),
                                                        in1=u2[:].unsqueeze(1).broadcast_to([128, 2, 32]), op=ALU.add),
                       reads=["u1", "u2"], writes=["krb"])

            def chain_b(c, j):
                P.emit("pool", lambda e: e.tensor_tensor(out=xn[:, 0:8, :], in0=xn[:, 0:8, :],
                                                         in1=gqa[:].unsqueeze(1).broadcast_to([128, 8, 128]), op=ALU.mult),
                       reads=["xn", "gqa"], writes=["xn"])
                P.emit("pool", lambda e: e.tensor_tensor(out=xn[:, 8:10, :], in0=xn[:, 8:10, :],
                                                         in1=gka[:].unsqueeze(1).broadcast_to([128, 2, 128]), op=ALU.mult),
                       reads=["xn", "gka"], writes=["xn"])
                xv = xn[:].rearrange("p h (i two) -> p h i two", two=2)
                qv = qkb[:].rearrange("p h (i two) -> p h i two", two=2)
                xe, xo = xv[:, :, :, 0], xv[:, :, :, 1]
                cb_ = tcA[:, j, :].unsqueeze(1).broadcast_to([128, 10, 64])
                sb_ = tsA[:, j, :].unsqueeze(1).broadcast_to([128, 10, 64])
                P.emit("pool", lambda e: e.tensor_tensor(out=t1[:], in0=xe, in1=cb_, op=ALU.mult), reads=["xn", "tab"], writes=["t1"])
                P.emit("dve", lambda e: e.tensor_tensor(out=t2[:], in0=xo, in1=sb_, op=ALU.mult), reads=["xn", "tab"], writes=["t2"])
                P.emit("dve", lambda e: e.tensor_tensor(out=qv[:, :, :, 0], in0=t1[:], in1=t2[:], op=ALU.subtract), reads=["t1", "t2"], writes=["qkb"])
                P.emit("pool", lambda e: e.tensor_tensor(out=t1[:], in0=xe, in1=sb_, op=ALU.mult), reads=["xn", "tab"], writes=["t1"])
                P.emit("dve", lambda e: e.tensor_tensor(out=t2[:], in0=xo, in1=cb_, op=ALU.mult), reads=["xn", "tab"], writes=["t2"])
                P.emit("dve", lambda e: e.tensor_tensor(out=qv[:, :, :, 1], in0=t1[:], in1=t2[:], op=ALU.add), reads=["t1", "t2"], writes=["qkb"])

            def out_t(c, j):
                js = slice(j * 128, (j + 1) * 128)
                transpose_tiles([ckvn[:, i * 128:(i + 1) * 128] for i in range(4)] + [krb[:].rearrange("p a i two -> p (a i two)")],
                                ["ckvn", "krb"],
                                [(ckvT[:, :, js], "ckvT", 0, 4, 128, "dve"), (krT[:, js].unsqueeze(1), "krT", 4, 1, 128, "act")], b=tr_bank())
                transpose_tiles([qkb[:, h, :] for h in range(8)], ["qkb"],
                                [(qaT[:, :, js], "qaT", 0, 8, 128, "act")], b=tr_bank())
                transpose_tiles([qkb[:, 8, :], qkb[:, 9, :]] + [cqn[:, i * 128:(i + 1) * 128] for i in range(4)], ["qkb", "cqn"],
                                [(kaT[:, :, js], "kaT", 0, 2, 128, "dve"), (cqT[:, :, js], "cqT", 2, 4, 128, "act")], b=tr_bank())

            load_x(0)
            for c in range(4):
                cs = slice(c * 512, (c + 1) * 512)
                P.dma_batch("sp", [(tcA[:], cosA[cs, :].rearrange("(j p) i -> p j i", p=128), "tab"),
                                   (tsA[:], sinA[cs, :].rearrange("(j p) i -> p j i", p=128), "tab"),
                                   (tcB[:], cosB[cs, :].rearrange("(j p) i -> p j i", p=128), "tab"),
                                   (tsB[:], sinB[cs, :].rearrange("(j p) i -> p j i", p=128), "tab")], "d_tab")
                front(c, 0)
                evac(c, 0)
                for j in range(4):
                    if j + 1 < 4:
                        front(c, j + 1)
                    chain_a(c, j)
                    if j + 1 < 4:
                        evac(c, j + 1)
                    chain_b(c, j)
                    out_t(c, j)
                P.dma("sp", QA.rearrange("h p t -> p h t")[:, :, cs], qaT[:], reads=["qaT"], src="d_s0")
                P.dma("sp", KA.rearrange("h p t -> p h t")[:, :, cs], kaT[:], reads=["kaT"], src="d_s1")
                P.dma("sp", KR[:, cs], krT[:], reads=["krT"], src="d_s2")
                P.emit("dve", lambda e: e.memset(barA[:], 0.0), writes=PSTK + ["barA"])
                for which in range(2):
                    wt, wkey, actT, akey, dst, stride = ((wuq, "wuq", cqT, "cqT", QBN, 192), (wukv, "wukv", ckvT, "ckvT", KBN, 256))[which]
                    for h in range(8):
                        b = next_bank()

                        def fn(e, h=h, b=b, wt=wt, actT=actT, stride=stride):
                            ins = None
                            for k in range(4):
                                ins = e.matmul(bank(b), lhsT=wt[:, k, h * stride:h * stride + 128], rhs=actT[:, k, :],
                                               start=(k == 0), stop=(k == 3))
                            return ins
                        P.emit("pe", fn, reads=[wkey, akey], writes=[pk(b)])
                        if h % 2:
                            P.emit("act", lambda e, h=h, b=b: e.copy(out=nopeT[:, h, :], in_=bank(b)), reads=[pk(b)] + PSTK, writes=[NOPEK[h]])
                        else:
                            P.emit("dve", lambda e, h=h, b=b: e.tensor_copy(out=nopeT[:, h, :], in_=bank(b)), reads=[pk(b)] + PSTK, writes=[NOPEK[h]])
                    P.dma("sp", dst.rearrange("h p t -> p h t")[:, :, cs], nopeT, reads=NOPEK, src="d_s3")
                for j in range(4):
                    tt = c * 4 + j
                    js = slice(j * 128, (j + 1) * 128)
                    b = next_bank()

                    def fn(e, b=b, js=js):
                        ins = None
                        for k in range(4):
                            ins = e.matmul(bank(b).rearrange("p (h c) -> p h c", c=64), lhsT=cqT[:, k, js],
                                           rhs=wuq[:, k, :].rearrange("p (h c) -> p h c", c=192)[:, :, 128:192],
                                           start=(k == 0), stop=(k == 3))
                        return ins
                    P.emit("pe", fn, reads=["cqT", "wuq"], writes=[pk(b)])
                    pv = bank(b).rearrange("p (h i two) -> p h i two", two=2, i=32)
                    pe_, po_ = pv[:, :, :, 0], pv[:, :, :, 1]
                    cB8 = tcB[:, j, :].unsqueeze(1).broadcast_to([128, 8, 32])
                    sB8 = tsB[:, j, :].unsqueeze(1).broadcast_to([128, 8, 32])
                    P.emit("dve", lambda e, a=pe_, t=cB8: e.tensor_tensor(out=v1[:], in0=a, in1=t, op=ALU.mult), reads=[pk(b), "tab"], writes=["v1"])
                    P.emit("dve", lambda e, a=po_, t=sB8: e.tensor_tensor(out=v2[:], in0=a, in1=t, op=ALU.mult), reads=[pk(b), "tab"], writes=["v2"])
                    P.emit("pool", lambda e: e.tensor_tensor(out=qrb[:, :, :, 0], in0=v1[:], in1=v2[:], op=ALU.subtract), reads=["v1", "v2"], writes=["qrb"])
                    P.emit("dve", lambda e, a=pe_, t=sB8: e.tensor_tensor(out=v1[:], in0=a, in1=t, op=ALU.mult), reads=[pk(b), "tab"], writes=["v1"])
                    P.emit("dve", lambda e, a=po_, t=cB8: e.tensor_tensor(out=v2[:], in0=a, in1=t, op=ALU.mult), reads=[pk(b), "tab"], writes=["v2"])
                    P.emit("pool", lambda e: e.tensor_tensor(out=qrb[:, :, :, 1], in0=v1[:], in1=v2[:], op=ALU.add), reads=["v1", "v2"], writes=["qrb"])
                    qrf = qrb[:].rearrange("p h i two -> p (h i two)")
                    transpose_tiles([qrf[:, i * 128:(i + 1) * 128] for i in range(4)], ["qrb"],
                                    [(qbrT[:, :, js], "qbrT", 0, 4, 128, "act")])
                    for hh in range(2):
                        b = next_bank()

                        def fn(e, b=b, js=js, hh=hh):
                            ins = None
                            for k in range(4):
                                ins = e.matmul(bank(b).rearrange("p (h c) -> p h c", c=128), lhsT=ckvT[:, k, js],
                                               rhs=wukv[:, k, :].rearrange("p (h c) -> p h c", c=256)[:, hh * 4:hh * 4 + 4, 128:256],
                                               start=(k == 0), stop=(k == 3))
                            return ins
                        P.emit("pe", fn, reads=["ckvT", "wukv"], writes=[pk(b)])
                        if hh:
                            P.emit("act", lambda e, b=b: e.copy(out=vbb[:, 512:1024], in_=bank(b)), reads=[pk(b)], writes=["vbb"])
                        else:
                            P.emit("dve", lambda e, b=b: e.tensor_copy(out=vbb[:, 0:512], in_=bank(b)), reads=[pk(b)], writes=["vbb"])
                    P.dma("sp", VB[tt * 128:(tt + 1) * 128, :], vbb[:], reads=["vbb"], src="d_s4")
                P.dma("sp", QBR.rearrange("h p t -> p h t")[:, :, cs], qbrT[:], reads=["qbrT"], src="d_s5")
            P.mute = False
            P.fence()
            P.build()

        es_a.close()
        es_bd = ExitStack()
        es_bc = ExitStack()
        if LAST_PHASE >= "B":
            sbd = lambda n, s, d: es_bd.enter_context(nc.sbuf_tensor("sb_" + n, s, d))
            wqm_t = sbd("wqm", [128, 16, 512], BF16)
            wom_t = sbd("wom", [128, 4, D], BF16)
            gtD_t = sbd("gtD", [128, D], F32); btD_t = sbd("btD", [128, D], F32)
            sbc = lambda n, s, d: es_bc.enter_context(nc.sbuf_tensor("sb_" + n, s, d))
            wout_t = sbc("wout", [128, 16, D], BF16)
            gtC_t = sbc("gtC", [128, D], F32); btC_t = sbc("btC", [128, D], F32)
        if LAST_PHASE >= "B":
          with ExitStack() as es:
            sb = lambda n, s, d: es.enter_context(nc.sbuf_tensor("sb_" + n, s, d))
            Kt = [sb("Kt%d" % i, [128, S], BF16) for i in range(2)]
            Vt = [sb("Vt%d" % i, [128, 16, 128], BF16) for i in range(2)]
            Qt = [sb("Qt%d" % i, [128, S], BF16) for i in range(2)]
            QRt = [sb("QRt%d" % i, [128, S], BF16) for i in range(2)]
            KRt = sb("KRt", [128, S], BF16)
            NPT = 6
            PT = [sb("PT%d" % i, [128, 512], BF16) for i in range(NPT)]
            rec = [sb("rec%d" % i, [128, 512], F32) for i in range(2)]
            ob = [sb("ob%d" % i, [128, 512], BF16) for i in range(2)]
            P.dma("sp", KRt[:], KR, writes=["KRt"], src="d_kr")
            wsrc_o = w_out.rearrange("(k p) c -> p k c", p=128)
            for k in range(16):
                P.dma("pool", wout_t[:, k, :], wsrc_o[:, k, :], src="d_pc")
            P.dma("pool", wqm_t[:], w_q_mem.rearrange("(k p) c -> p k c", p=128), src="d_pc")
            P.dma("pool", wom_t[:], w_o_mem.rearrange("(k p) c -> p k c", p=128), src="d_pc")
            P.dma("sp", gtC_t[:], ln1_g.partition_broadcast(128), src="d_pg")
            P.dma("sp", btC_t[:], ln1_b.partition_broadcast(128), src="d_pg")
            P.dma("sp", gtD_t[:], ln2_g.partition_broadcast(128), src="d_pg")
            P.dma("sp", btD_t[:], ln2_b.partition_broadcast(128), src="d_pg")
            w1src = w_mlp_in.rearrange("(k p) c -> p k c", p=128)
            w2src = w_mlp_out.rearrange("(f p) c -> p f c", p=128)
            for fg in range(32):
                P.dma("pool", W1s[fg], w1src[:, :, fg * 256:(fg + 1) * 256], src="d_pc")
            for dh in range(2):
                for fq in range(16):
                    P.dma("pool", W2s[dh, fq], w2src[:, fq * 4:(fq + 1) * 4, dh * 1024:(dh + 1) * 1024], src="d_pc")
            blk = [0]
            ptc = [0]
            kvc = [0]

            def attn_head(hidx, Kap, kkey, Vap, vkey, Qap, qkey, scale, mla, po=0, qr=None, qrkey=None):
                for qc in range(4):
                    qs = slice(qc * 512, (qc + 1) * 512)
                    bi = blk[0] % 2
                    blk[0] += 1
                    ob_, db_ = 4 + 2 * bi, 5 + 2 * bi

                    def emit_s(st):
                        sbk = st % 4
                        ss_ = slice(st * 128, (st + 1) * 128)

                        def fn(e):
                            ins = e.matmul(bank(sbk), lhsT=Kap[:, ss_], rhs=Qap[:, qs], start=True, stop=not mla)
                            if mla:
                                ins = e.matmul(bank(sbk), lhsT=KRt[po:po + 64, ss_], rhs=qr[po:po + 64, qs], start=False, stop=True)
                            return ins
                        rd = [kkey, qkey] + (["KRt", qrkey] if mla else [])
                        P.emit("pe", fn, reads=rd, writes=[pk(sbk)])
                    emit_s(0)
                    emit_s(1)
                    emit_s(2)
                    for st in range(16):
                        sbk = st % 4
                        pi = ptc[0] % NPT
                        ptc[0] += 1
                        P.emit("act", lambda e, sbk=sbk, pi=pi: e.activation(out=PT[pi][:], in_=bank(sbk), func=AF.Exp, scale=scale),
                               reads=[pk(sbk)], writes=["PT%d" % pi])
                        if st + 3 < 16:
                            emit_s(st + 3)

                        def fn(e, st=st, pi=pi):
                            e.matmul(bank(ob_), lhsT=Vap[:, st, :], rhs=PT[pi][:], start=(st == 0), stop=(st == 15))
                            return e.matmul(bank(db_), lhsT=ones[:], rhs=PT[pi][:], start=(st == 0), stop=(st == 15))
                        P.emit("pe", fn, reads=[vkey, "PT%d" % pi, "ones"], writes=[pk(ob_), pk(db_)])
                    P.emit("dve", lambda e: e.reciprocal(out=rec[bi][:], in_=bank(db_)), reads=[pk(db_)], writes=["rec%d" % bi])
                    P.emit("dve", lambda e: e.tensor_tensor(out=ob[bi][:], in0=bank(ob_), in1=rec[bi][:], op=ALU.mult),
                           reads=[pk(ob_), "rec%d" % bi], writes=["ob%d" % bi])
                    P.dma("sp", OT[hidx][:, qs], ob[bi][:], reads=["ob%d" % bi], src="d_o%d" % bi)

            qcnt = [0]
            for kv in range(2):
                ki = kvc[0] % 2
                kvc[0] += 1
                P.dma("sp", Kt[ki][:], KA[kv], writes=["Kt%d" % ki], src="d_k%d" % ki)
                P.dma("sp", Vt[ki][:], VA.rearrange("(n p) c -> p n c", p=128)[:, :, kv * 128:(kv + 1) * 128],
                      writes=["Vt%d" % ki], src="d_v%d" % ki)
                for g in range(4):
                    h = kv * 4 + g
                    qi = qcnt[0] % 2
                    qcnt[0] += 1
                    P.dma("sp", Qt[qi][:], QA[h], writes=["Qt%d" % qi], src="d_q%d" % qi)
                    attn_head(h, Kt[ki], "Kt%d" % ki, Vt[ki], "Vt%d" % ki, Qt[qi], "Qt%d" % qi, SC_A, False)
            for h in range(8):
                ki = kvc[0] % 2
                kvc[0] += 1
                P.dma("sp", Kt[ki][:], KBN[h], writes=["Kt%d" % ki], src="d_k%d" % ki)
                P.dma("sp", Vt[ki][:], VB.rearrange("(n p) c -> p n c", p=128)[:, :, h * 128:(h + 1) * 128],
                      writes=["Vt%d" % ki], src="d_v%d" % ki)
                qi = qcnt[0] % 2
                qcnt[0] += 1
                P.dma("sp", Qt[qi][:], QBN[h], writes=["Qt%d" % qi], src="d_q%d" % qi)
                ri = (h // 2) % 2
                if h % 2 == 0:
                    P.dma("sp", QRt[ri][:], QBR[h // 2], writes=["QRt%d" % ri], src="d_qr%d" % ri)
                attn_head(8 + h, Kt[ki], "Kt%d" % ki, Vt[ki], "Vt%d" % ki, Qt[qi], "Qt%d" % qi, SC_B, True,
                          po=(h % 2) * 64, qr=QRt[ri], qrkey="QRt%d" % ri)
            P.fence()
            P.build()

        def ln_scratch(sb, tag):
            return (sb("st6" + tag, [128, 4, 6], F32), sb("mv" + tag, [128, 2], F32), sb("sd" + tag, [128, 1], F32),
                    sb("rstd" + tag, [128, 1], F32), sb("nb" + tag, [128, 1], F32))

        if LAST_PHASE >= "C":
          with ExitStack() as es:
            sb = lambda n, s, d: es.enter_context(nc.sbuf_tensor("sb_" + n, s, d))
            wout = wout_t
            gt = gtC_t; bt = btC_t
            OTc = [sb("OTc%d" % i, [128, 16, 512], BF16) for i in range(2)]
            xt = [sb("xtC%d" % i, [128, D], F32) for i in range(2)]
            xp = [sb("xpC%d" % i, [128, D], F32) for i in range(2)]
            lns = ln_scratch(sb, "C")
            osrc = OT.rearrange("k p t -> p k t")

            def load_ot(c):
                P.dma("sp", OTc[c % 2][:], osrc[:, :, c * 512:(c + 1) * 512], writes=["OTc%d" % (c % 2)], src="d_ot%d" % (c % 2))

            def load_xc(tt):
                P.dma("sp", xt[tt % 2][:], x[tt * 128:(tt + 1) * 128, :], writes=["xt%d" % (tt % 2)], src="d_x%d" % (tt % 2))
            load_ot(0)
            load_xc(0)
            for c in range(4):
                if c + 1 < 4:
                    load_ot(c + 1)
                for j in range(4):
                    tt = c * 4 + j
                    js = slice(j * 128, (j + 1) * 128)
                    if tt + 1 < 16:
                        load_xc(tt + 1)
                    pi = tt % 2
                    bks = [pi * 4 + i for i in range(4)]

                    def fn(e, c=c, js=js, pi=pi):
                        ins = None
                        for k in range(16):
                            for dc in range(4):
                                ins = e.matmul(bank(pi * 4 + dc), lhsT=OTc[c % 2][:, k, js],
                                               rhs=wout[:, k, dc * 512:(dc + 1) * 512], start=(k == 0), stop=(k == 15))
                        return ins
                    P.emit("pe", fn, reads=["OTc%d" % (c % 2), "wout"], writes=[pk(b) for b in bks])
                    xk = "xp%d" % pi
                    for dc in range(4):
                        dsl = slice(dc * 512, (dc + 1) * 512)
                        P.emit("dve", lambda e, pi=pi, tt=tt, dc=dc, dsl=dsl: e.scalar_tensor_tensor(
                            out=xp[pi][:, dsl], in0=xt[tt % 2][:, dsl], scalar=ALPHA, in1=bank(pi * 4 + dc), op0=ALU.mult, op1=ALU.add),
                            reads=["xt%d" % (tt % 2), pk(pi * 4 + dc)], writes=[xk])
                    layernorm(xp[pi][:], xk, gt, bt, "lC", *lns)
                    P.dma("sp", X1[tt * 128:(tt + 1) * 128, :], xp[pi][:], reads=[xk], src="d_so%d" % pi)
            P.fence()
            P.build()

        es_bc.close()

        if LAST_PHASE >= "D":
          with ExitStack() as es:
            sb = lambda n, s, d: es.enter_context(nc.sbuf_tensor("sb_" + n, s, d))
            wq = wqm_t
            wo = wom_t
            gt = gtD_t; bt = btD_t
            x1c = [sb("x1c%d" % i, [128, 4, D], F32) for i in range(2)]
            xb = sb("xbD", [128, D], BF16)
            x1T = sb("x1T", [128, 16, 512], BF16)
            qmT = sb("qmT", [128, 4, 512], BF16)
            omT = sb("omT", [128, 4, 512], BF16)
            PT = [sb("PTD%d" % i, [128, 512], BF16) for i in range(4)]
            rec = sb("recD", [128, 512], F32)
            lns = ln_scratch(sb, "D")

            def load_x1(c):
                P.dma_batch("sp", [(x1c[c % 2][:, j, :], X1[(c * 4 + j) * 128:(c * 4 + j + 1) * 128, :], "x1c%d_%d" % (c % 2, j))
                                   for j in range(4)], "d_x%d" % (c % 2))
            load_x1(0)
            ptc = [0]
            def d_front(c):
                xc = x1c[c % 2]
                for j in range(4):
                    js = slice(j * 128, (j + 1) * 128)
                    xk = "x1c%d_%d" % (c % 2, j)
                    P.emit("dve", lambda e, xc=xc, j=j: e.tensor_copy(out=xb[:], in_=xc[:, j, :]), reads=[xk], writes=["xb"])
                    for half in range(2):
                        transpose_tiles([xb[:, (half * 8 + i) * 128:(half * 8 + i + 1) * 128] for i in range(8)], ["xb"],
                                        [(x1T[:, half * 8:half * 8 + 8, js], "x1T", 0, 8, 128, "act" if half else "dve")])
                for h in range(4):
                    b = next_bank()

                    def fn(e, h=h, b=b):
                        ins = None
                        for k in range(16):
                            ins = e.matmul(bank(b), lhsT=wq[:, k, h * 128:(h + 1) * 128], rhs=x1T[:, k, :], start=(k == 0), stop=(k == 15))
                        return ins
                    P.emit("pe", fn, reads=["wqm", "x1T"], writes=[pk(b)])
                    P.emit("act", lambda e, h=h, b=b: e.copy(out=qmT[:, h, :], in_=bank(b)), reads=[pk(b)], writes=["qmT%d" % h])
            def d_mid(c):
                xc = x1c[c % 2]
                for h in range(4):
                    sbs = [next_bank(), next_bank()]
                    obk, dbk = next_bank(), next_bank()
                    pis = []
                    for m in range(2):
                        P.emit("pe", lambda e, h=h, m=m, b=sbs[m]: e.matmul(bank(b), lhsT=kmT[:, h, m * 128:(m + 1) * 128], rhs=qmT[:, h, :],
                                                                           start=True, stop=True),
                               reads=["kmT", "qmT%d" % h], writes=[pk(sbs[m])])
                        pi = ptc[0] % 4
                        ptc[0] += 1
                        pis.append(pi)
                        P.emit("act", lambda e, pi=pi, b=sbs[m]: e.activation(out=PT[pi][:], in_=bank(b), func=AF.Exp, scale=SC_M),
                               reads=[pk(sbs[m])], writes=["PTD%d" % pi])
                    for m in range(2):
                        def fn(e, h=h, m=m, pi=pis[m], obk=obk, dbk=dbk):
                            e.matmul(bank(obk), lhsT=vm[:, m, h * 128:(h + 1) * 128], rhs=PT[pi][:], start=(m == 0), stop=(m == 1))
                            return e.matmul(bank(dbk), lhsT=ones[:], rhs=PT[pi][:], start=(m == 0), stop=(m == 1))
                        P.emit("pe", fn, reads=["vm", "ones", "PTD%d" % pis[m]], writes=[pk(obk), pk(dbk)])
                    P.emit("dve", lambda e, dbk=dbk: e.reciprocal(out=rec[:], in_=bank(dbk)), reads=[pk(dbk)], writes=["recD"])
                    P.emit("dve", lambda e, h=h, obk=obk: e.tensor_tensor(out=omT[:, h, :], in0=bank(obk), in1=rec[:], op=ALU.mult),
                           reads=[pk(obk), "recD"], writes=["omT%d" % h])
            def d_back(c):
                xc = x1c[c % 2]
                for j in range(4):
                    tt = c * 4 + j
                    js = slice(j * 128, (j + 1) * 128)
                    pi = tt % 2
                    bks = [pi * 4 + i for i in range(4)]

                    def fn(e, js=js, pi=pi):
                        ins = None
                        for k in range(4):
                            for dc in range(4):
                                ins = e.matmul(bank(pi * 4 + dc), lhsT=omT[:, k, js],
                                               rhs=wo[:, k, dc * 512:(dc + 1) * 512], start=(k == 0), stop=(k == 3))
                        return ins
                    P.emit("pe", fn, reads=["omT%d" % h for h in range(4)] + ["wom"], writes=[pk(b) for b in bks])
                    xk = "x1c%d_%d" % (c % 2, j)
                    for dc in range(4):
                        dsl = slice(dc * 512, (dc + 1) * 512)
                        P.emit("dve", lambda e, xc=xc, j=j, pi=pi, dc=dc, dsl=dsl: e.scalar_tensor_tensor(
                            out=xc[:, j, dsl], in0=xc[:, j, dsl], scalar=ALPHA, in1=bank(pi * 4 + dc), op0=ALU.mult, op1=ALU.add),
                            reads=[xk, pk(pi * 4 + dc)], writes=[xk])
                    layernorm(xc[:, j, :], xk, gt, bt, "lD", *lns)
                    P.dma("sp", X2[tt * 128:(tt + 1) * 128, :], xc[:, j, :], reads=[xk], src="d_so%d_%d" % (c % 2, j))
            d_front(0)
            for c in range(4):
                if c + 1 < 4:
                    load_x1(c + 1)
                d_mid(c)
                if c + 1 < 4:
                    d_front(c + 1)
                d_back(c)
            P.fence()
            P.build()

        es_bd.close()

        if LAST_PHASE >= "E":
          with ExitStack() as es:
            sb = lambda n, s, d: es.enter_context(nc.sbuf_tensor("sb_" + n, s, d))
            hT = sb("hT", [128, 64, 512], BF16)
            x2T = [sb("x2T0", [128, 16, 512], BF16)] * 2
            xs = sb("xsE", [128, D], F32)
            xb = sb("xbE", [128, D], BF16)
            NWB = 3
            w1b = [sb("w1b%d" % i, [128, 16, 256], BF16) for i in range(NWB)]
            w2b = [sb("w2b%d" % i, [128, 4, 1024], BF16) for i in range(NWB)]
            gt = sb("gtE", [128, D], F32); bt = sb("btE", [128, D], F32)
            xr = [sb("xrE%d" % i, [128, 1024], F32) for i in range(2)]
            xp4 = sb("xp4", [128, 4, D], F32)
            rbf = [sb("rbf%d" % i, [128, 512], BF16) for i in range(4)]
            lns = ln_scratch(sb, "E")
            P.dma("sp", gt[:], ln3_g.partition_broadcast(128), writes=["lng"], src="d_g")
            P.dma("sp", bt[:], ln3_b.partition_broadcast(128), writes=["lnb"], src="d_b")
            w1src = w_mlp_in.rearrange("(k p) c -> p k c", p=128)
            w2src = w_mlp_out.rearrange("(f p) c -> p f c", p=128)
            w1c = [0]
            w2c = [0]
            rc = [0]
            xrc = [0]

            def make_x2T(c):
                for j in range(4):
                    tt = c * 4 + j
                    js = slice(j * 128, (j + 1) * 128)
                    P.dma("sp", xs[:], X2[tt * 128:(tt + 1) * 128, :], writes=["xsE"], src="d_x0")
                    P.emit("dve", lambda e: e.tensor_copy(out=xb[:], in_=xs[:]), reads=["xsE"], writes=["xb"])
                    for half in range(2):
                        transpose_tiles([xb[:, (half * 8 + i) * 128:(half * 8 + i + 1) * 128] for i in range(8)], ["xb"],
                                        [(x2T[c % 2][:, half * 8:half * 8 + 8, js], "x2T0", 0, 8, 128, "act" if half else "dve")])

            make_x2T(0)
            for c in range(4):
                xT_ = x2T[c % 2]
                xTk = "x2T0"
                for fg in range(32):
                    wi = w1c[0] % NWB
                    w1c[0] += 1
                    P.dma("pool", w1b[wi][:], W1s[fg], writes=["w1b%d" % wi], src="d_w1%d" % wi)
                    for fi in range(2):
                        ft = fg * 2 + fi
                        b = next_bank()

                        def fn(e, wi=wi, fi=fi, b=b, xT_=xT_):
                            ins = None
                            for k in range(16):
                                ins = e.matmul(bank(b), lhsT=w1b[wi][:, k, fi * 128:(fi + 1) * 128], rhs=xT_[:, k, :], start=(k == 0), stop=(k == 15))
                            return ins
                        P.emit("pe", fn, reads=["w1b%d" % wi, xTk], writes=[pk(b)])
                        ri = rc[0] % 4
                        rc[0] += 1
                        P.emit("act", lambda e, ri=ri, b=b: e.activation(out=rbf[ri][:], in_=bank(b), func=AF.Relu), reads=[pk(b)], writes=["rbf%d" % ri])
                        P.emit("dve", lambda e, ri=ri, ft=ft: e.tensor_tensor(out=hT[:, ft, :], in0=rbf[ri][:], in1=rbf[ri][:], op=ALU.mult),
                               reads=["rbf%d" % ri], writes=["hT%d" % (ft // 4)])
                if c + 1 < 4:
                    make_x2T(c + 1)
                for dh in range(2):
                    for fq in range(16):
                        wi = w2c[0] % NWB
                        w2c[0] += 1
                        P.dma("pool", w2b[wi][:], W2s[dh, fq], writes=["w2b%d" % wi], src="d_w2%d" % wi)

                        def fn(e, wi=wi, fq=fq):
                            ins = None
                            for fi in range(4):
                                ft = fq * 4 + fi
                                for j in range(4):
                                    for dcl in range(2):
                                        ins = e.matmul(bank(j * 2 + dcl), lhsT=hT[:, ft, j * 128:(j + 1) * 128],
                                                       rhs=w2b[wi][:, fi, dcl * 512:(dcl + 1) * 512], start=(ft == 0), stop=(ft == 63))
                            return ins
                        P.emit("pe", fn, reads=["w2b%d" % wi, "hT%d" % fq], writes=[pk(b) for b in range(8)])
                    for j in range(4):
                        tt = c * 4 + j
                        xi = xrc[0] % 2
                        xrc[0] += 1
                        P.dma("sp", xr[xi][:], X2[tt * 128:(tt + 1) * 128, dh * 1024:(dh + 1) * 1024], writes=["xr%d" % xi], src="d_xr%d" % xi)
                        for dcl in range(2):
                            P.emit("dve", lambda e, j=j, xi=xi, dh=dh, dcl=dcl: e.scalar_tensor_tensor(
                                out=xp4[:, j, dh * 1024 + dcl * 512:dh * 1024 + (dcl + 1) * 512], in0=xr[xi][:, dcl * 512:(dcl + 1) * 512],
                                scalar=ALPHA, in1=bank(j * 2 + dcl), op0=ALU.mult, op1=ALU.add),
                                reads=["xr%d" % xi, pk(j * 2 + dcl)], writes=["xp4_%d" % j])
                for j in range(4):
                    tt = c * 4 + j
                    layernorm(xp4[:, j, :], "xp4_%d" % j, gt, bt, "lE", *lns)
                    P.dma("sp", out[tt * 128:(tt + 1) * 128, :], xp4[:, j, :], reads=["xp4_%d" % j], src="d_out%d" % j)
            P.fence()
            P.build()
    return nc


def _rope_tables():
    f32 = np.float32
    t = np.arange(S)
    row = (t // 64).astype(f32)
    col = (t % 64).astype(f32)

    def ang(rot_dim):
        quarter = rot_dim // 4
        inv = (f32(10000.0) ** (-np.arange(quarter, dtype=f32) / f32(quarter))).astype(f32)
        return np.concatenate([row[:, None] * inv[None, :], col[:, None] * inv[None, :]], axis=-1).astype(f32)
    a, b = ang(128), ang(64)
    return (np.cos(a).astype(f32), np.sin(a).astype(f32), np.cos(b).astype(f32), np.sin(b).astype(f32))


_NC_CACHE = {}


def kernel(**inputs):
    f32 = np.float32
    if "nc" not in _NC_CACHE:
        _NC_CACHE["nc"] = build_program()
    nc = _NC_CACHE["nc"]
    cA, sA, cB, sB = _rope_tables()
    shared = {}
    for k in ("w_in", "w_uq", "w_ukv", "w_out", "w_q_mem", "w_kv_mem", "w_o_mem", "w_mlp_in", "w_mlp_out"):
        shared[k] = np.ascontiguousarray(np.asarray(inputs[k], dtype=f32)[0])
    for k in ("g_qa", "g_ka", "g_cq", "g_ckv", "ln1_g", "ln1_b", "mem_ln_g", "mem_ln_b", "ln2_g", "ln2_b", "ln3_g", "ln3_b"):
        shared[k] = np.ascontiguousarray(np.asarray(inputs[k], dtype=f32)[0][None, :])
    shared.update(cosA=cA, sinA=sA, cosB=cB, sinB=sB, ident=np.eye(128, dtype=f32))
    xs = np.asarray(inputs["x"], dtype=f32)
    ms = np.asarray(inputs["mem"], dtype=f32)
    in_maps = []
    for b in range(NCORES):
        m = dict(shared)
        m["x"] = np.ascontiguousarray(xs[b])
        m["mem"] = np.ascontiguousarray(ms[b])
        in_maps.append(m)
    ncores = getattr(kernel, "debug_cores", NCORES)
    in_maps = in_maps[:ncores]
    res = run_bass_kernel_spmd(nc, in_maps, core_ids=list(range(ncores)))
    if DEBUG_OUT:
        kernel.last = res.results
    outs = [np.asarray(r["out"], dtype=f32) for r in res.results]
    return np.stack(outs, axis=0)
```

```python
import os
import types
import numpy as np
from contextlib import ExitStack
import concourse.bass as bass
import concourse.mybir as mybir
from concourse.bass_utils import run_bass_kernel_spmd

F32 = mybir.dt.float32
BF16 = mybir.dt.bfloat16
AF = mybir.ActivationFunctionType
ALU = mybir.AluOpType
AX = mybir.AxisListType

S = 2048
D = 2048
MEM = 256
NCORES = 8
IN_COLS = 2624
DFF = 8192
ALPHA = 2.0 ** 0.25
LN_EPS = 1e-5
RMS_EPS = 1e-6
SC_A = 128.0 ** -0.5
SC_B = 192.0 ** -0.5
SC_M = 128.0 ** -0.5

DEBUG_OUT = False
LAST_PHASE = "E"
DBG = 99


def _freeze(fn):
    if getattr(fn, "__closure__", None) is None:
        return fn
    cells = []
    for c in fn.__closure__:
        try:
            cells.append(types.CellType(c.cell_contents))
        except ValueError:
            cells.append(c)
    g = types.FunctionType(fn.__code__, fn.__globals__, fn.__name__, fn.__defaults__, tuple(cells))
    g.__kwdefaults__ = fn.__kwdefaults__
    return g


class Prog:
    ENG = {"pe": "tensor", "act": "scalar", "dve": "vector", "pool": "gpsimd", "sp": "sync"}

    def __init__(self, nc, es):
        self.nc = nc
        self.es = es
        self.ops = {e: [] for e in self.ENG}
        self.sems = {}
        self.cnt = {}
        self.unit = {}
        self.clock = {e: {} for e in self.ENG}
        self.evclock = {}
        self.lastw = {}
        self.readers = {}
        for e in self.ENG:
            self.source(e, 1)

    def source(self, name, unit=16):
        if name not in self.sems:
            self.sems[name] = self.es.enter_context(self.nc.semaphore("s_" + name))
            self.cnt[name] = 0
            self.unit[name] = unit
        return name

    def _need(self, eng, ev, waits):
        if ev is None:
            return
        src, c = ev
        if src == eng == "pe":
            return
        ck = self.clock[eng]
        if ck.get(src, 0) >= c:
            return
        waits[src] = max(waits.get(src, 0), c)
        for s, v in self.evclock[ev].items():
            if ck.get(s, 0) < v:
                ck[s] = v

    mute = False

    def emit(self, eng, fn, reads=(), writes=(), src=None):
        if self.mute:
            return None
        waits = {}
        writes = list(writes) + [b for b in reads if b.startswith("ps")]
        reads = [b for b in reads if not b.startswith("ps")]
        for b in reads:
            self._need(eng, self.lastw.get(b), waits)
        for b in writes:
            self._need(eng, self.lastw.get(b), waits)
            for ev in self.readers.get(b, ()):
                self._need(eng, ev, waits)
        if src is None:
            src = eng
        self.cnt[src] += 1
        ev = (src, self.cnt[src])
        ck = dict(self.clock[eng])
        ck[src] = self.cnt[src]
        self.evclock[ev] = ck
        for b in reads:
            self.readers.setdefault(b, []).append(ev)
        for b in writes:
            self.lastw[b] = ev
            self.readers[b] = []
        self.ops[eng].append((sorted(waits.items()), _freeze(fn), src))
        if os.environ.get("DBGTRACE"):
            print("EMIT", eng, ev, "waits", sorted(waits.items()), "R", list(reads), "W", list(writes))
        return ev

    def dma(self, q, out, in_, reads=(), writes=(), src=None):
        self.source(src, 16)
        return self.emit(q, lambda e: e.dma_start(out=out, in_=in_), reads=reads, writes=writes, src=src)

    def dma_batch(self, q, items, src, reads=()):
        self.source(src, 16)
        keys = [k for (_, _, k) in items]
        ev = None
        for n, (o, i, k) in enumerate(items):
            ev = self.emit(q, lambda e, o=o, i=i: e.dma_start(out=o, in_=i), reads=reads if n == 0 else (),
                           writes=keys if n == 0 else (), src=src)
        if ev is None:
            return None
        for k in keys:
            self.lastw[k] = ev
            self.readers[k] = []
        return ev

    def fence(self):
        for e in self.ENG:
            waits = []
            for s, c in self.cnt.items():
                if c > 0 and self.clock[e].get(s, 0) < c:
                    waits.append((s, c))
                    self.clock[e][s] = c
            self.ops[e].append((sorted(waits), None, None))
        self.lastw = {}
        self.readers = {}
        self.evclock = {}

    def build(self):
        with self.nc.Block() as block:
            for e, hname in self.ENG.items():
                ops = self.ops[e]

                def body(eng, ops=ops):
                    for waits, fn, src in ops:
                        for s, c in waits:
                            eng.wait_ge(self.sems[s], c * self.unit[s])
                        if fn is not None:
                            fn(eng).then_inc(self.sems[src], self.unit[src])

                getattr(block, hname)(body)
        self.ops = {e: [] for e in self.ENG}


def build_program():
    nc = bass.Bass("TRN2", target_bir_lowering=False)

    def din(name, shape, dt=F32):
        return nc.dram_tensor(name, shape, dt, kind="ExternalInput").ap()

    def dscr(name, shape, dt):
        kind = "ExternalOutput" if DEBUG_OUT else "Internal"
        return nc.dram_tensor(name, shape, dt, kind=kind).ap()

    x = din("x", [S, D])
    mem = din("mem", [MEM, D])
    w_in = din("w_in", [D, IN_COLS])
    g_qa = din("g_qa", [1, 128]); g_ka = din("g_ka", [1, 128])
    g_cq = din("g_cq", [1, 512]); g_ckv = din("g_ckv", [1, 512])
    w_uq = din("w_uq", [512, 1536]); w_ukv = din("w_ukv", [512, 2048])
    w_out = din("w_out", [D, D])
    ln1_g = din("ln1_g", [1, D]); ln1_b = din("ln1_b", [1, D])
    mem_ln_g = din("mem_ln_g", [1, D]); mem_ln_b = din("mem_ln_b", [1, D])
    w_q_mem = din("w_q_mem", [D, 512]); w_kv_mem = din("w_kv_mem", [D, 1024]); w_o_mem = din("w_o_mem", [512, D])
    ln2_g = din("ln2_g", [1, D]); ln2_b = din("ln2_b", [1, D])
    w_mlp_in = din("w_mlp_in", [D, DFF]); w_mlp_out = din("w_mlp_out", [DFF, D])
    ln3_g = din("ln3_g", [1, D]); ln3_b = din("ln3_b", [1, D])
    cosA = din("cosA", [S, 64]); sinA = din("sinA", [S, 64])
    cosB = din("cosB", [S, 32]); sinB = din("sinB", [S, 32])
    ident_in = din("ident", [128, 128])
    out = nc.dram_tensor("out", [S, D], F32, kind="ExternalOutput").ap()

    QA = dscr("QA", [8, 128, S], BF16); KA = dscr("KA", [2, 128, S], BF16); VA = dscr("VA", [S, 256], BF16)
    QBN = dscr("QBN", [8, 128, S], BF16); QBR = dscr("QBR", [4, 128, S], BF16)
    KBN = dscr("KBN", [8, 128, S], BF16); KR = dscr("KR", [128, S], BF16); VB = dscr("VB", [S, 1024], BF16)
    OT = dscr("OT", [16, 128, S], BF16)
    X1 = dscr("X1", [S, D], F32); X2 = dscr("X2", [S, D], F32)
    W1s = nc.dram_tensor("W1s", [32, 128, 16, 256], BF16).ap()
    W2s = nc.dram_tensor("W2s", [2, 16, 128, 4, 1024], BF16).ap()

    with ExitStack() as ges:
        P = Prog(nc, ges)

        def gsb(name, shape, dt):
            return ges.enter_context(nc.sbuf_tensor("sb_" + name, shape, dt))

        ident = gsb("ident", [128, 128], BF16)
        ones = gsb("ones", [128, 128], BF16)
        eps_ln = gsb("eps_ln", [128, 1], F32)
        eps_rms = gsb("eps_rms", [128, 1], F32)
        kmT = gsb("kmT", [128, 4, 256], BF16)
        vm = gsb("vm", [128, 2, 512], BF16)
        psum = [ges.enter_context(nc.psum_tensor("psb%d" % i, [128, 512], F32)) for i in range(8)]

        def bank(i):
            return psum[i][:, :]

        def bankbf(i):
            return bank(i).bitcast(BF16)

        def pk(i):
            return "ps%d" % i

        bank_rr = [0]

        def next_bank():
            b = bank_rr[0]
            bank_rr[0] = (b + 1) % 8
            return b

        def layernorm(xp, xkey, gt, bt, pfx, st6, mv, sd, rstd, nb, gb_eng="pool"):
            for i in range(4):
                P.emit("dve", lambda e, i=i: e.bn_stats(out=st6[:, i, :], in_=xp[:, i * 512:(i + 1) * 512]),
                       reads=[xkey], writes=[pfx + "st%d" % i])
            P.emit("dve", lambda e: e.bn_aggr(out=mv[:], in_=st6[:]),
                   reads=[pfx + "st%d" % i for i in range(4)], writes=[pfx + "mv"])
            P.emit("act", lambda e: e.activation(out=sd[:], in_=mv[:, 1:2], func=AF.Sqrt, bias=eps_ln[:, 0:1], scale=1.0),
                   reads=[pfx + "mv", "eps"], writes=[pfx + "sd"])
            P.emit("dve", lambda e: e.reciprocal(out=rstd[:], in_=sd[:]), reads=[pfx + "sd"], writes=[pfx + "rstd"])
            P.emit("dve", lambda e: e.scalar_tensor_tensor(out=nb[:], in0=mv[:, 0:1], scalar=-1.0, in1=rstd[:],
                                                           op0=ALU.mult, op1=ALU.mult),
                   reads=[pfx + "mv", pfx + "rstd"], writes=[pfx + "nb"])
            P.emit("act", lambda e: e.activation(out=xp, in_=xp, func=AF.Identity, scale=rstd[:, 0:1], bias=nb[:, 0:1]),
                   reads=[xkey, pfx + "rstd", pfx + "nb"], writes=[xkey])
            P.emit(gb_eng, lambda e: e.tensor_tensor(out=xp, in0=xp, in1=gt[:], op=ALU.mult),
                   reads=[xkey, "lng"], writes=[xkey])
            P.emit(gb_eng, lambda e: e.tensor_tensor(out=xp, in0=xp, in1=bt[:], op=ALU.add),
                   reads=[xkey, "lnb"], writes=[xkey])

        def transpose_tiles(src_tiles, src_keys, dsts, b=None):
            if b is None:
                b = next_bank()
            pb = bankbf(b)

            def fn(e):
                ins = None
                for i, t in enumerate(src_tiles):
                    n = t.shape[-1]
                    ins = e.transpose(out=pb[0:n, i * 128:(i + 1) * 128], in_=t, identity=ident[:])
                return ins
            P.emit("pe", fn, reads=list(src_keys) + ["ident"], writes=[pk(b)])
            for dst, dkey, first, cnt, rows, eng in dsts:
                srcv = pb[0:rows, first * 128:(first + cnt) * 128].rearrange("p (c t) -> p c t", t=128)
                if eng == "act":
                    P.emit("act", lambda e, d=dst, s=srcv: e.copy(out=d, in_=s), reads=[pk(b)], writes=[dkey])
                else:
                    P.emit("dve", lambda e, d=dst, s=srcv: e.tensor_copy(out=d, in_=s), reads=[pk(b)], writes=[dkey])

        with ExitStack() as es:
            idf = es.enter_context(nc.sbuf_tensor("idf", [128, 128], F32))
            P.dma("sp", idf[:], ident_in, writes=["idf"], src="d_misc")
            P.emit("dve", lambda e: e.tensor_copy(out=ident[:], in_=idf[:]), reads=["idf"], writes=["ident"])
            P.emit("dve", lambda e: e.memset(ones[:], 1.0), writes=["ones"])
            P.emit("dve", lambda e: e.memset(eps_ln[:], LN_EPS), writes=["eps"])
            P.emit("dve", lambda e: e.memset(eps_rms[:], RMS_EPS), writes=["eps"])
            P.fence()
            P.build()

        es_a = ExitStack()
        if LAST_PHASE >= "A":
            sba = lambda n, s, d: es_a.enter_context(nc.sbuf_tensor("sb_" + n, s, d))
            win_t = sba("win", [128, 16, IN_COLS], BF16)
            wuq_t = sba("wuq", [128, 4, 1536], BF16)
            wukv_t = sba("wukv", [128, 4, 2048], BF16)

        if not os.environ.get("SKIP0"):
          with ExitStack() as es:
              sb = lambda n, s, d: es.enter_context(nc.sbuf_tensor("sb_" + n, s, d))
              wkv = sb("wkv", [128, 16, 1024], BF16)
              gt = sb("gt0", [128, D], F32); bt = sb("bt0", [128, D], F32)
              mt_ = [sb("memt%d" % i, [128, D], F32) for i in range(2)]
              mb = sb("memb", [128, D], BF16)
              memT = sb("memT", [128, 16, 256], BF16)
              st6 = sb("st6_0", [128, 4, 6], F32); mv = sb("mv_0", [128, 2], F32)
              sd = sb("sd_0", [128, 1], F32); rstd = sb("rstd_0", [128, 1], F32); nb = sb("nb_0", [128, 1], F32)
              wsrc = w_kv_mem.rearrange("(k p) c -> p k c", p=128)
              P.dma_batch("pool", [(wkv[:, 4 * i:4 * i + 4, :], wsrc[:, 4 * i:4 * i + 4, :], "wkv") for i in range(4)], "d_w0")
              if LAST_PHASE >= "A":
                  wsrc_i = w_in.rearrange("(k p) c -> p k c", p=128)
                  for k in range(16):
                      P.dma("pool", win_t[:, k, :], wsrc_i[:, k, :], src="d_pa")
                  P.dma("pool", wuq_t[:], w_uq.rearrange("(k p) c -> p k c", p=128), src="d_pa")
                  P.dma("pool", wukv_t[:], w_ukv.rearrange("(k p) c -> p k c", p=128), src="d_pa")
              P.dma("sp", gt[:], mem_ln_g.partition_broadcast(128), writes=["lng"], src="d_g")
              P.dma("sp", bt[:], mem_ln_b.partition_broadcast(128), writes=["lnb"], src="d_b")
              for m in range(2):
                  P.dma("sp", mt_[m][:], mem[m * 128:(m + 1) * 128, :], writes=["memt%d" % m], src="d_x%d" % m)
              for m in range(2):
                  layernorm(mt_[m][:], "memt%d" % m, gt, bt, "l0", st6, mv, sd, rstd, nb)
                  P.emit("dve", lambda e, m=m: e.tensor_copy(out=mb[:], in_=mt_[m][:]), reads=["memt%d" % m], writes=["memb"])
                  for half in range(2):
                      transpose_tiles([mb[:, (half * 8 + i) * 128:(half * 8 + i + 1) * 128] for i in range(8)], ["memb"],
                                      [(memT[:, half * 8:half * 8 + 8, m * 128:(m + 1) * 128], "memT", 0, 8, 128, "act")])
              for h in range(4):
                  b = next_bank()

                  def fn(e, h=h, b=b):
                      ins = None
                      for k in range(16):
                          ins = e.matmul(bank(b)[:, 0:256], lhsT=wkv[:, k, h * 128:(h + 1) * 128], rhs=memT[:, k, :],
                                         start=(k == 0), stop=(k == 15))
                      return ins
                  P.emit("pe", fn, reads=["wkv", "memT"], writes=[pk(b)])
                  P.emit("act", lambda e, h=h, b=b: e.copy(out=kmT[:, h, :], in_=bank(b)[:, 0:256]), reads=[pk(b)], writes=["kmT"])
              for m in range(2):
                  b = next_bank()

                  def fn(e, m=m, b=b):
                      ins = None
                      for k in range(16):
                          ins = e.matmul(bank(b), lhsT=memT[:, k, m * 128:(m + 1) * 128], rhs=wkv[:, k, 512:1024],
                                         start=(k == 0), stop=(k == 15))
                      return ins
                  P.emit("pe", fn, reads=["wkv", "memT"], writes=[pk(b)])
                  P.emit("dve", lambda e, m=m, b=b: e.tensor_copy(out=vm[:, m, :], in_=bank(b)), reads=[pk(b)], writes=["vm"])
              P.fence()
              P.build()

        if LAST_PHASE >= "A":
          with ExitStack() as es:
            sb = lambda n, s, d: es.enter_context(nc.sbuf_tensor("sb_" + n, s, d))
            win = win_t
            wuq = wuq_t
            wukv = wukv_t
            gqa = sb("gqa", [128, 128], F32); gka = sb("gka", [128, 128], F32)
            gcq = sb("gcq", [128, 512], F32); gckv = sb("gckv", [128, 512], F32)
            tcA = sb("tcA", [128, 4, 64], F32); tsA = sb("tsA", [128, 4, 64], F32)
            tcB = sb("tcB", [128, 4, 32], F32); tsB = sb("tsB", [128, 4, 32], F32)
            xt = [sb("xtA%d" % i, [128, D], F32) for i in range(2)]
            xb = sb("xbA", [128, D], BF16)
            xT = sb("xTA", [128, 16, 128], BF16)
            ssq = sb("ssq", [128, 12], F32); sdq = sb("sdq", [128, 12], F32); rsq = sb("rsq", [128, 12], F32)
            sqj = sb("sqj", [128, 512], F32)
            xn = sb("xn", [128, 10, 128], F32)
            t1 = sb("t1", [128, 10, 64], F32); t2 = sb("t2", [128, 10, 64], F32)
            qkb = sb("qkb", [128, 10, 128], BF16)
            cqn = sb("cqn", [128, 512], BF16); ckvn = sb("ckvn", [128, 512], BF16)
            krb = sb("krb", [128, 2, 32, 2], BF16)
            u1 = sb("u1", [128, 32], F32); u2 = sb("u2", [128, 32], F32)
            vab = sb("vab", [128, 256], BF16)
            qaT = sb("qaT", [128, 8, 512], BF16); kaT = sb("kaT", [128, 2, 512], BF16)
            cqT = sb("cqT", [128, 4, 512], BF16); ckvT = sb("ckvT", [128, 4, 512], BF16)
            krT = sb("krT", [128, 512], BF16)
            pst = sb("pst", [128, IN_COLS], F32)
            nopeT = pst[:, 0:2048].bitcast(BF16).rearrange("p (h t) -> p h t", t=512)
            PSTK = ["pst%d" % i for i in range(6)]
            NOPEK = ["nopeT%d" % i for i in range(8)]
            barA = sb("barA", [128, 1], F32)
            qbrT = sb("qbrT", [128, 4, 512], BF16)
            qrb = sb("qrb", [128, 8, 32, 2], BF16)
            v1 = sb("v1", [128, 8, 32], F32); v2 = sb("v2", [128, 8, 32], F32)
            vbb = sb("vbb", [128, 1024], BF16)

            aux = os.environ.get("NOAUX", "")
            P.mute = "g" in aux
            P.dma_batch("sp", [(gqa[:], g_qa.partition_broadcast(128), "gqa"), (gka[:], g_ka.partition_broadcast(128), "gka"),
                               (gcq[:], g_cq.partition_broadcast(128), "gcq"), (gckv[:], g_ckv.partition_broadcast(128), "gckv")], "d_g")

            P.mute = False

            def load_x(tt):
                P.dma("sp", xt[tt % 2][:], x[tt * 128:(tt + 1) * 128, :], writes=["xt%d" % (tt % 2)], src="d_x%d" % (tt % 2))

            GROUPS = [(0, 512), (512, 1024), (1024, 1536), (1536, 2048), (2048, 2560), (2560, 2624)]
            trb = [0]

            def tr_bank():
                trb[0] ^= 1
                return 6 + trb[0]

            def front(c, j):
                tt = c * 4 + j
                xk = "xt%d" % (tt % 2)
                xtt = xt[tt % 2]
                if tt + 1 < 16:
                    load_x(tt + 1)
                P.emit("dve", lambda e: e.tensor_copy(out=xb[:], in_=xtt[:]), reads=[xk], writes=["xb"])
                for half in range(2):
                    transpose_tiles([xb[:, (half * 8 + i) * 128:(half * 8 + i + 1) * 128] for i in range(8)], ["xb"],
                                    [(xT[:, half * 8:half * 8 + 8, :], "xT", 0, 8, 128, "act" if half else "dve")], b=tr_bank())
                for gi, (lo, hi) in enumerate(GROUPS):
                    def fn(e, lo=lo, hi=hi, gi=gi):
                        ins = None
                        for k in range(16):
                            ins = e.matmul(bank(gi)[:, 0:hi - lo], lhsT=xT[:, k, :], rhs=win[:, k, lo:hi],
                                           start=(k == 0), stop=(k == 15))
                        return ins
                    P.emit("pe", fn, reads=["xT", "win"], writes=[pk(gi)])

            def evac(c, j):
                for gi, (lo, hi) in enumerate(GROUPS):
                    if gi % 2 == 0:
                        P.emit("act", lambda e, gi=gi, lo=lo, hi=hi: e.copy(out=pst[:, lo:hi], in_=bank(gi)[:, 0:hi - lo]),
                               reads=[pk(gi)], writes=[PSTK[gi]] + (NOPEK if j == 0 else []))
                    else:
                        P.emit("dve", lambda e, gi=gi, lo=lo, hi=hi: e.tensor_copy(out=pst[:, lo:hi], in_=bank(gi)[:, 0:hi - lo]),
                               reads=[pk(gi)], writes=[PSTK[gi]] + (NOPEK if j == 0 else []))

            def chain_a(c, j):
                tt = c * 4 + j
                P.emit("dve", lambda e: e.memset(ssq[:], 0.0), writes=["ssq%d" % i for i in range(12)])
                for hh in range(10):
                    P.emit("act", lambda e, hh=hh: e.activation(out=sqj[:, 0:128], in_=pst[:, hh * 128:(hh + 1) * 128],
                                                                func=AF.Square, accum_out=ssq[:, hh:hh + 1]),
                           reads=[PSTK[hh // 4]], writes=["sqj", "ssq%d" % hh])
                for i, b in ((10, 3), (11, 4)):
                    P.emit("act", lambda e, i=i, b=b: e.activation(out=sqj[:], in_=pst[:, 1536 + (b - 3) * 512:2048 + (b - 3) * 512],
                                                                   func=AF.Square, accum_out=ssq[:, i:i + 1]),
                           reads=[PSTK[b]], writes=["sqj", "ssq%d" % i])
                P.emit("pool", lambda e: e.tensor_copy(out=vab[:], in_=pst[:, 1280:1536]), reads=[PSTK[2]], writes=["vab"])
                P.dma("sp", VA[tt * 128:(tt + 1) * 128, :], vab[:], reads=["vab"], src="d_va")
                P.emit("act", lambda e: e.activation(out=sdq[:, 0:10], in_=ssq[:, 0:10], func=AF.Sqrt, bias=eps_rms[:, 0:1], scale=1.0 / 128),
                       reads=["ssq%d" % i for i in range(10)] + ["eps"], writes=["sdq_a"])
                P.emit("act", lambda e: e.activation(out=sdq[:, 10:12], in_=ssq[:, 10:12], func=AF.Sqrt, bias=eps_rms[:, 0:1], scale=1.0 / 512),
                       reads=["ssq10", "ssq11", "eps"], writes=["sdq_b"])
                P.emit("dve", lambda e: e.reciprocal(out=rsq[:], in_=sdq[:]), reads=["sdq_a", "sdq_b"], writes=["rsq"])
                P.emit("dve", lambda e: e.tensor_tensor(
                    out=xn[:], in0=pst[:, 0:1280].rearrange("p (h d) -> p h d", d=128),
                    in1=rsq[:, 0:10].unsqueeze(2).broadcast_to([128, 10, 128]), op=ALU.mult),
                    reads=[PSTK[0], PSTK[1], PSTK[2], "rsq"], writes=["xn"])
                P.emit("dve", lambda e: e.scalar_tensor_tensor(out=cqn[:], in0=pst[:, 1536:2048], scalar=rsq[:, 10:11], in1=gcq[:], op0=ALU.mult, op1=ALU.mult),
                       reads=[PSTK[3], "rsq", "gcq"], writes=["cqn"])
                P.emit("dve", lambda e: e.scalar_tensor_tensor(out=ckvn[:], in0=pst[:, 2048:2560], scalar=rsq[:, 11:12], in1=gckv[:], op0=ALU.mult, op1=ALU.mult),
                       reads=[PSTK[4], "rsq", "gckv"], writes=["ckvn"])
                kv5 = pst[:, 2560:2624].rearrange("p (i two) -> p i two", two=2)
                ke, ko = kv5[:, :, 0], kv5[:, :, 1]
                cB = tcB[:, j, :]
                sB = tsB[:, j, :]
                P.emit("dve", lambda e: e.tensor_tensor(out=u1[:], in0=ke, in1=cB, op=ALU.mult), reads=[PSTK[5], "tab"], writes=["u1"])
                P.emit("dve", lambda e: e.tensor_tensor(out=u2[:], in0=ko, in1=sB, op=ALU.mult), reads=[PSTK[5], "tab"], writes=["u2"])
                P.emit("dve", lambda e: e.tensor_tensor(out=krb[:, :, :, 0], in0=u1[:].unsqueeze(1).broadcast_to([128, 2, 32]),
                                                        in1=u2[:].unsqueeze(1).broadcast_to([128, 2, 32]), op=ALU.subtract),
                       reads=["u1", "u2"], writes=["krb"])
                P.emit("dve", lambda e: e.tensor_tensor(out=u1[:], in0=ke, in1=sB, op=ALU.mult), reads=[PSTK[5], "tab"], writes=["u1"])
                P.emit("dve", lambda e: e.tensor_tensor(out=u2[:], in0=ko, in1=cB, op=ALU.mult), reads=[PSTK[5], "tab"], writes=["u2"])
                P.emit("dve", lambda e: e.tensor_tensor(out=krb[:, :, :, 1], in0=u1[:].unsqueeze(1).broadcast_to([128, 2, 32]),
                                                        in1=u2[:].unsqueeze(1).broadcast_to([128, 2, 32]), op=ALU.add),
                       reads=["u1", "u2"], writes=["krb"])

            def chain_b(c, j):
                P.emit("dve", lambda e: e.tensor_tensor(out=xn[:, 0:8, :], in0=xn[:, 0:8, :],
                                                         in1=gqa[:].unsqueeze(1).broadcast_to([128, 8, 128]), op=ALU.mult),
                       reads=["xn", "gqa"], writes=["xn"])
                P.emit("dve", lambda e: e.tensor_tensor(out=xn[:, 8:10, :], in0=xn[:, 8:10, :],
                                                         in1=gka[:].unsqueeze(1).broadcast_to([128, 2, 128]), op=ALU.mult),
                       reads=["xn", "gka"], writes=["xn"])
                xv = xn[:].rearrange("p h (i two) -> p h i two", two=2)
                qv = qkb[:].rearrange("p h (i two) -> p h i two", two=2)
                xe, xo = xv[:, :, :, 0], xv[:, :, :, 1]
                cb_ = tcA[:, j, :].unsqueeze(1).broadcast_to([128, 10, 64])
                sb_ = tsA[:, j, :].unsqueeze(1).broadcast_to([128, 10, 64])
                P.emit("dve", lambda e: e.tensor_tensor(out=t1[:], in0=xe, in1=cb_, op=ALU.mult), reads=["xn", "tab"], writes=["t1"])
                P.emit("dve", lambda e: e.tensor_tensor(out=t2[:], in0=xo, in1=sb_, op=ALU.mult), reads=["xn", "tab"], writes=["t2"])
                P.emit("dve", lambda e: e.tensor_tensor(out=qv[:, :, :, 0], in0=t1[:], in1=t2[:], op=ALU.subtract), reads=["t1", "t2"], writes=["qkb"])
                P.emit("dve", lambda e: e.tensor_tensor(out=t1[:], in0=xe, in1=sb_, op=ALU.mult), reads=["xn", "tab"], writes=["t1"])
                P.emit("dve", lambda e: e.tensor_tensor(out=t2[:], in0=xo, in1=cb_, op=ALU.mult), reads=["xn", "tab"], writes=["t2"])
                P.emit("dve", lambda e: e.tensor_tensor(out=qv[:, :, :, 1], in0=t1[:], in1=t2[:], op=ALU.add), reads=["t1", "t2"], writes=["qkb"])

            def out_t(c, j):
                js = slice(j * 128, (j + 1) * 128)
                transpose_tiles([ckvn[:, i * 128:(i + 1) * 128] for i in range(4)] + [krb[:].rearrange("p a i two -> p (a i two)")],
                                ["ckvn", "krb"],
                                [(ckvT[:, :, js], "ckvT", 0, 4, 128, "dve"), (krT[:, js].unsqueeze(1), "krT", 4, 1, 128, "act")], b=tr_bank())
                transpose_tiles([qkb[:, h, :] for h in range(8)], ["qkb"],
                                [(qaT[:, :, js], "qaT", 0, 8, 128, "act")], b=tr_bank())
                transpose_tiles([qkb[:, 8, :], qkb[:, 9, :]] + [cqn[:, i * 128:(i + 1) * 128] for i in range(4)], ["qkb", "cqn"],
                                [(kaT[:, :, js], "kaT", 0, 2, 128, "dve"), (cqT[:, :, js], "cqT", 2, 4, 128, "act")], b=tr_bank())

            load_x(0)
            for c in range(4):
                cs = slice(c * 512, (c + 1) * 512)
                P.dma_batch("sp", [(tcA[:], cosA[cs, :].rearrange("(j p) i -> p j i", p=128), "tab"),
                                   (tsA[:], sinA[cs, :].rearrange("(j p) i -> p j i", p=128), "tab"),
                                   (tcB[:], cosB[cs, :].rearrange("(j p) i -> p j i", p=128), "tab"),
                                   (tsB[:], sinB[cs, :].rearrange("(j p) i -> p j i", p=128), "tab")], "d_tab")
                front(c, 0)
                evac(c, 0)
                for j in range(4):
                    if j + 1 < 4:
                        front(c, j + 1)
                    chain_a(c, j)
                    if j + 1 < 4:
                        evac(c, j + 1)
                    chain_b(c, j)
                    out_t(c, j)
                P.dma("sp", QA.rearrange("h p t -> p h t")[:, :, cs], qaT[:], reads=["qaT"], src="d_s0")
                P.dma("sp", KA.rearrange("h p t -> p h t")[:, :, cs], kaT[:], reads=["kaT"], src="d_s1")
                P.dma("sp", KR[:, cs], krT[:], reads=["krT"], src="d_s2")
                P.emit("dve", lambda e: e.memset(barA[:], 0.0), writes=PSTK + ["barA"])
                for which in range(2):
                    wt, wkey, actT, akey, dst, stride = ((wuq, "wuq", cqT, "cqT", QBN, 192), (wukv, "wukv", ckvT, "ckvT", KBN, 256))[which]
                    for h in range(8):
                        b = next_bank()

                        def fn(e, h=h, b=b, wt=wt, actT=actT, stride=stride):
                            ins = None
                            for k in range(4):
                                ins = e.matmul(bank(b), lhsT=wt[:, k, h * stride:h * stride + 128], rhs=actT[:, k, :],
                                               start=(k == 0), stop=(k == 3))
                            return ins
                        P.emit("pe", fn, reads=[wkey, akey], writes=[pk(b)])
                        if h % 2:
                            P.emit("act", lambda e, h=h, b=b: e.copy(out=nopeT[:, h, :], in_=bank(b)), reads=[pk(b)] + PSTK, writes=[NOPEK[h]])
                        else:
                            P.emit("dve", lambda e, h=h, b=b: e.tensor_copy(out=nopeT[:, h, :], in_=bank(b)), reads=[pk(b)] + PSTK, writes=[NOPEK[h]])
                    P.dma("sp", dst.rearrange("h p t -> p h t")[:, :, cs], nopeT, reads=NOPEK, src="d_s3")
                for j in range(4):
                    tt = c * 4 + j
                    js = slice(j * 128, (j + 1) * 128)
                    b = next_bank()

                    def fn(e, b=b, js=js):
                        ins = None
                        for k in range(4):
                            ins = e.matmul(bank(b).rearrange("p (h c) -> p h c", c=64), lhsT=cqT[:, k, js],
                                           rhs=wuq[:, k, :].rearrange("p (h c) -> p h c", c=192)[:, :, 128:192],
                                           start=(k == 0), stop=(k == 3))
                        return ins
                    P.emit("pe", fn, reads=["cqT", "wuq"], writes=[pk(b)])
                    pv = bank(b).rearrange("p (h i two) -> p h i two", two=2, i=32)
                    pe_, po_ = pv[:, :, :, 0], pv[:, :, :, 1]
                    cB8 = tcB[:, j, :].unsqueeze(1).broadcast_to([128, 8, 32])
                    sB8 = tsB[:, j, :].unsqueeze(1).broadcast_to([128, 8, 32])
                    P.emit("dve", lambda e, a=pe_, t=cB8: e.tensor_tensor(out=v1[:], in0=a, in1=t, op=ALU.mult), reads=[pk(b), "tab"], writes=["v1"])
                    P.emit("dve", lambda e, a=po_, t=sB8: e.tensor_tensor(out=v2[:], in0=a, in1=t, op=ALU.mult), reads=[pk(b), "tab"], writes=["v2"])
                    P.emit("pool", lambda e: e.tensor_tensor(out=qrb[:, :, :, 0], in0=v1[:], in1=v2[:], op=ALU.subtract), reads=["v1", "v2"], writes=["qrb"])
                    P.emit("dve", lambda e, a=pe_, t=sB8: e.tensor_tensor(out=v1[:], in0=a, in1=t, op=ALU.mult), reads=[pk(b), "tab"], writes=["v1"])
                    P.emit("dve", lambda e, a=po_, t=cB8: e.tensor_tensor(out=v2[:], in0=a, in1=t, op=ALU.mult), reads=[pk(b), "tab"], writes=["v2"])
                    P.emit("pool", lambda e: e.tensor_tensor(out=qrb[:, :, :, 1], in0=v1[:], in1=v2[:], op=ALU.add), reads=["v1", "v2"], writes=["qrb"])
                    qrf = qrb[:].rearrange("p h i two -> p (h i two)")
                    transpose_tiles([qrf[:, i * 128:(i + 1) * 128] for i in range(4)], ["qrb"],
                                    [(qbrT[:, :, js], "qbrT", 0, 4, 128, "act")])
                    for hh in range(2):
                        b = next_bank()

                        def fn(e, b=b, js=js, hh=hh):
                            ins = None
                            for k in range(4):
                                ins = e.matmul(bank(b).rearrange("p (h c) -> p h c", c=128), lhsT=ckvT[:, k, js],
                                               rhs=wukv[:, k, :].rearrange("p (h c) -> p h c", c=256)[:, hh * 4:hh * 4 + 4, 128:256],
                                               start=(k == 0), stop=(k == 3))
                            return ins
                        P.emit("pe", fn, reads=["ckvT", "wukv"], writes=[pk(b)])
                        if hh:
                            P.emit("act", lambda e, b=b: e.copy(out=vbb[:, 512:1024], in_=bank(b)), reads=[pk(b)], writes=["vbb"])
                        else:
                            P.emit("dve", lambda e, b=b: e.tensor_copy(out=vbb[:, 0:512], in_=bank(b)), reads=[pk(b)], writes=["vbb"])
                    P.dma("sp", VB[tt * 128:(tt + 1) * 128, :], vbb[:], reads=["vbb"], src="d_s4")
                P.dma("sp", QBR.rearrange("h p t -> p h t")[:, :, cs], qbrT[:], reads=["qbrT"], src="d_s5")
            P.mute = False
            P.fence()
            P.build()

        es_a.close()
        es_bd = ExitStack()
        es_bc = ExitStack()
        if LAST_PHASE >= "B":
            sbd = lambda n, s, d: es_bd.enter_context(nc.sbuf_tensor("sb_" + n, s, d))
            wqm_t = sbd("wqm", [128, 16, 512], BF16)
            wom_t = sbd("wom", [128, 4, D], BF16)
            gtD_t = sbd("gtD", [128, D], F32); btD_t = sbd("btD", [128, D], F32)
            sbc = lambda n, s, d: es_bc.enter_context(nc.sbuf_tensor("sb_" + n, s, d))
            wout_t = sbc("wout", [128, 16, D], BF16)
            gtC_t = sbc("gtC", [128, D], F32); btC_t = sbc("btC", [128, D], F32)
        if LAST_PHASE >= "B":
          with ExitStack() as es:
            sb = lambda n, s, d: es.enter_context(nc.sbuf_tensor("sb_" + n, s, d))
            Kt = [sb("Kt%d" % i, [128, S], BF16) for i in range(2)]
            Vt = [sb("Vt%d" % i, [128, 16, 128], BF16) for i in range(2)]
            Qt = [sb("Qt%d" % i, [128, S], BF16) for i in range(2)]
            QRt = [sb("QRt%d" % i, [128, S], BF16) for i in range(2)]
            KRt = sb("KRt", [128, S], BF16)
            NPT = 6
            PT = [sb("PT%d" % i, [128, 512], BF16) for i in range(NPT)]
            rec = [sb("rec%d" % i, [128, 512], F32) for i in range(2)]
            ob = [sb("ob%d" % i, [128, 512], BF16) for i in range(2)]
            P.dma("sp", KRt[:], KR, writes=["KRt"], src="d_kr")
            wsrc_o = w_out.rearrange("(k p) c -> p k c", p=128)
            for k in range(16):
                P.dma("pool", wout_t[:, k, :], wsrc_o[:, k, :], src="d_pc")
            P.dma("pool", wqm_t[:], w_q_mem.rearrange("(k p) c -> p k c", p=128), src="d_pc")
            P.dma("pool", wom_t[:], w_o_mem.rearrange("(k p) c -> p k c", p=128), src="d_pc")
            P.dma("sp", gtC_t[:], ln1_g.partition_broadcast(128), src="d_pg")
            P.dma("sp", btC_t[:], ln1_b.partition_broadcast(128), src="d_pg")
            P.dma("sp", gtD_t[:], ln2_g.partition_broadcast(128), src="d_pg")
            P.dma("sp", btD_t[:], ln2_b.partition_broadcast(128), src="d_pg")
            w1src = w_mlp_in.rearrange("(k p) c -> p k c", p=128)
            w2src = w_mlp_out.rearrange("(f p) c -> p f c", p=128)
            for fg in range(32):
                P.dma("pool", W1s[fg], w1src[:, :, fg * 256:(fg + 1) * 256], src="d_pc")
            for dh in range(2):
                for fq in range(16):
                    P.dma("pool", W2s[dh, fq], w2src[:, fq * 4:(fq + 1) * 4, dh * 1024:(dh + 1) * 1024], src="d_pc")
            blk = [0]
            ptc = [0]
            kvc = [0]

            def attn_head(hidx, Kap, kkey, Vap, vkey, Qap, qkey, scale, mla, po=0, qr=None, qrkey=None):
                for qc in range(4):
                    qs = slice(qc * 512, (qc + 1) * 512)
                    bi = blk[0] % 2
                    blk[0] += 1
                    ob_, db_ = 4 + 2 * bi, 5 + 2 * bi

                    def emit_s(st):
                        sbk = st % 4
                        ss_ = slice(st * 128, (st + 1) * 128)

                        def fn(e):
                            ins = e.matmul(bank(sbk), lhsT=Kap[:, ss_], rhs=Qap[:, qs], start=True, stop=not mla)
                            if mla:
                                ins = e.matmul(bank(sbk), lhsT=KRt[po:po + 64, ss_], rhs=qr[po:po + 64, qs], start=False, stop=True)
                            return ins
                        rd = [kkey, qkey] + (["KRt", qrkey] if mla else [])
                        P.emit("pe", fn, reads=rd, writes=[pk(sbk)])
                    emit_s(0)
                    emit_s(1)
                    emit_s(2)
                    for st in range(16):
                        sbk = st % 4
                        pi = ptc[0] % NPT
                        ptc[0] += 1
                        P.emit("act", lambda e, sbk=sbk, pi=pi: e.activation(out=PT[pi][:], in_=bank(sbk), func=AF.Exp, scale=scale),
                               reads=[pk(sbk)], writes=["PT%d" % pi])
                        if st + 3 < 16:
                            emit_s(st + 3)

                        def fn(e, st=st, pi=pi):
                            e.matmul(bank(ob_), lhsT=Vap[:, st, :], rhs=PT[pi][:], start=(st == 0), stop=(st == 15))
                            return e.matmul(bank(db_), lhsT=ones[:], rhs=PT[pi][:], start=(st == 0), stop=(st == 15))
                        P.emit("pe", fn, reads=[vkey, "PT%d" % pi, "ones"], writes=[pk(ob_), pk(db_)])
                    P.emit("dve", lambda e: e.reciprocal(out=rec[bi][:], in_=bank(db_)), reads=[pk(db_)], writes=["rec%d" % bi])
                    P.emit("dve", lambda e: e.tensor_tensor(out=ob[bi][:], in0=bank(ob_), in1=rec[bi][:], op=ALU.mult),
                           reads=[pk(ob_), "rec%d" % bi], writes=["ob%d" % bi])
                    P.dma("sp", OT[hidx][:, qs], ob[bi][:], reads=["ob%d" % bi], src="d_o%d" % bi)

            qcnt = [0]
            for kv in range(2):
                ki = kvc[0] % 2
                kvc[0] += 1
                P.dma("sp", Kt[ki][:], KA[kv], writes=["Kt%d" % ki], src="d_k%d" % ki)
                P.dma("sp", Vt[ki][:], VA.rearrange("(n p) c -> p n c", p=128)[:, :, kv * 128:(kv + 1) * 128],
                      writes=["Vt%d" % ki], src="d_v%d" % ki)
                for g in range(4):
                    h = kv * 4 + g
                    qi = qcnt[0] % 2
                    qcnt[0] += 1
                    P.dma("sp", Qt[qi][:], QA[h], writes=["Qt%d" % qi], src="d_q%d" % qi)
                    attn_head(h, Kt[ki], "Kt%d" % ki, Vt[ki], "Vt%d" % ki, Qt[qi], "Qt%d" % qi, SC_A, False)
            for h in range(8):
                ki = kvc[0] % 2
                kvc[0] += 1
                P.dma("sp", Kt[ki][:], KBN[h], writes=["Kt%d" % ki], src="d_k%d" % ki)
                P.dma("sp", Vt[ki][:], VB.rearrange("(n p) c -> p n c", p=128)[:, :, h * 128:(h + 1) * 128],
                      writes=["Vt%d" % ki], src="d_v%d" % ki)
                qi = qcnt[0] % 2
                qcnt[0] += 1
                P.dma("sp", Qt[qi][:], QBN[h], writes=["Qt%d" % qi], src="d_q%d" % qi)
                ri = (h // 2) % 2
                if h % 2 == 0:
                    P.dma("sp", QRt[ri][:], QBR[h // 2], writes=["QRt%d" % ri], src="d_qr%d" % ri)
                attn_head(8 + h, Kt[ki], "Kt%d" % ki, Vt[ki], "Vt%d" % ki, Qt[qi], "Qt%d" % qi, SC_B, True,
                          po=(h % 2) * 64, qr=QRt[ri], qrkey="QRt%d" % ri)
            P.fence()
            P.build()

        def ln_scratch(sb, tag):
            return (sb("st6" + tag, [128, 4, 6], F32), sb("mv" + tag, [128, 2], F32), sb("sd" + tag, [128, 1], F32),
                    sb("rstd" + tag, [128, 1], F32), sb("nb" + tag, [128, 1], F32))

        if LAST_PHASE >= "C":
          with ExitStack() as es:
            sb = lambda n, s, d: es.enter_context(nc.sbuf_tensor("sb_" + n, s, d))
            wout = wout_t
            gt = gtC_t; bt = btC_t
            OTc = [sb("OTc%d" % i, [128, 16, 512], BF16) for i in range(2)]
            xt = [sb("xtC%d" % i, [128, D], F32) for i in range(2)]
            xp = [sb("xpC%d" % i, [128, D], F32) for i in range(2)]
            lns = ln_scratch(sb, "C")
            osrc = OT.rearrange("k p t -> p k t")

            def load_ot(c):
                P.dma("sp", OTc[c % 2][:], osrc[:, :, c * 512:(c + 1) * 512], writes=["OTc%d" % (c % 2)], src="d_ot%d" % (c % 2))

            def load_xc(tt):
                P.dma("sp", xt[tt % 2][:], x[tt * 128:(tt + 1) * 128, :], writes=["xt%d" % (tt % 2)], src="d_x%d" % (tt % 2))
            load_ot(0)
            load_xc(0)
            for c in range(4):
                if c + 1 < 4:
                    load_ot(c + 1)
                for j in range(4):
                    tt = c * 4 + j
                    js = slice(j * 128, (j + 1) * 128)
                    if tt + 1 < 16:
                        load_xc(tt + 1)
                    pi = tt % 2
                    bks = [pi * 4 + i for i in range(4)]

                    def fn(e, c=c, js=js, pi=pi):
                        ins = None
                        for k in range(16):
                            for dc in range(4):
                                ins = e.matmul(bank(pi * 4 + dc), lhsT=OTc[c % 2][:, k, js],
                                               rhs=wout[:, k, dc * 512:(dc + 1) * 512], start=(k == 0), stop=(k == 15))
                        return ins
                    P.emit("pe", fn, reads=["OTc%d" % (c % 2), "wout"], writes=[pk(b) for b in bks])
                    xk = "xp%d" % pi
                    for dc in range(4):
                        dsl = slice(dc * 512, (dc + 1) * 512)
                        P.emit("dve", lambda e, pi=pi, tt=tt, dc=dc, dsl=dsl: e.scalar_tensor_tensor(
                            out=xp[pi][:, dsl], in0=xt[tt % 2][:, dsl], scalar=ALPHA, in1=bank(pi * 4 + dc), op0=ALU.mult, op1=ALU.add),
                            reads=["xt%d" % (tt % 2), pk(pi * 4 + dc)], writes=[xk])
                    layernorm(xp[pi][:], xk, gt, bt, "lC", *lns)
                    P.dma("sp", X1[tt * 128:(tt + 1) * 128, :], xp[pi][:], reads=[xk], src="d_so%d" % pi)
            P.fence()
            P.build()

        es_bc.close()

        if LAST_PHASE >= "D":
          with ExitStack() as es:
            sb = lambda n, s, d: es.enter_context(nc.sbuf_tensor("sb_" + n, s, d))
            wq = wqm_t
            wo = wom_t
            gt = gtD_t; bt = btD_t
            x1c = [sb("x1c%d" % i, [128, 4, D], F32) for i in range(2)]
            xb = sb("xbD", [128, D], BF16)
            x1T = sb("x1T", [128, 16, 512], BF16)
            qmT = sb("qmT", [128, 4, 512], BF16)
            omT = sb("omT", [128, 4, 512], BF16)
            PT = [sb("PTD%d" % i, [128, 512], BF16) for i in range(4)]
            rec = sb("recD", [128, 512], F32)
            lns = ln_scratch(sb, "D")

            def load_x1(c):
                P.dma_batch("sp", [(x1c[c % 2][:, j, :], X1[(c * 4 + j) * 128:(c * 4 + j + 1) * 128, :], "x1c%d_%d" % (c % 2, j))
                                   for j in range(4)], "d_x%d" % (c % 2))
            load_x1(0)
            ptc = [0]
            def d_front(c):
                xc = x1c[c % 2]
                for j in range(4):
                    js = slice(j * 128, (j + 1) * 128)
                    xk = "x1c%d_%d" % (c % 2, j)
                    P.emit("dve", lambda e, xc=xc, j=j: e.tensor_copy(out=xb[:], in_=xc[:, j, :]), reads=[xk], writes=["xb"])
                    for half in range(2):
                        transpose_tiles([xb[:, (half * 8 + i) * 128:(half * 8 + i + 1) * 128] for i in range(8)], ["xb"],
                                        [(x1T[:, half * 8:half * 8 + 8, js], "x1T", 0, 8, 128, "act" if half else "dve")])
                for h in range(4):
                    b = next_bank()

                    def fn(e, h=h, b=b):
                        ins = None
                        for k in range(16):
                            ins = e.matmul(bank(b), lhsT=wq[:, k, h * 128:(h + 1) * 128], rhs=x1T[:, k, :], start=(k == 0), stop=(k == 15))
                        return ins
                    P.emit("pe", fn, reads=["wqm", "x1T"], writes=[pk(b)])
                    P.emit("act", lambda e, h=h, b=b: e.copy(out=qmT[:, h, :], in_=bank(b)), reads=[pk(b)], writes=["qmT%d" % h])
            def d_mid(c):
                xc = x1c[c % 2]
                for h in range(4):
                    sbs = [next_bank(), next_bank()]
                    obk, dbk = next_bank(), next_bank()
                    pis = []
                    for m in range(2):
                        P.emit("pe", lambda e, h=h, m=m, b=sbs[m]: e.matmul(bank(b), lhsT=kmT[:, h, m * 128:(m + 1) * 128], rhs=qmT[:, h, :],
                                                                           start=True, stop=True),
                               reads=["kmT", "qmT%d" % h], writes=[pk(sbs[m])])
                        pi = ptc[0] % 4
                        ptc[0] += 1
                        pis.append(pi)
                        P.emit("act", lambda e, pi=pi, b=sbs[m]: e.activation(out=PT[pi][:], in_=bank(b), func=AF.Exp, scale=SC_M),
                               reads=[pk(sbs[m])], writes=["PTD%d" % pi])
                    for m in range(2):
                        def fn(e, h=h, m=m, pi=pis[m], obk=obk, dbk=dbk):
                            e.matmul(bank(obk), lhsT=vm[:, m, h * 128:(h + 1) * 128], rhs=PT[pi][:], start=(m == 0), stop=(m == 1))
                            return e.matmul(bank(dbk), lhsT=ones[:], rhs=PT[pi][:], start=(m == 0), stop=(m == 1))
                        P.emit("pe", fn, reads=["vm", "ones", "PTD%d" % pis[m]], writes=[pk(obk), pk(dbk)])
                    P.emit("dve", lambda e, dbk=dbk: e.reciprocal(out=rec[:], in_=bank(dbk)), reads=[pk(dbk)], writes=["recD"])
                    P.emit("dve", lambda e, h=h, obk=obk: e.tensor_tensor(out=omT[:, h, :], in0=bank(obk), in1=rec[:], op=ALU.mult),
                           reads=[pk(obk), "recD"], writes=["omT%d" % h])
            def d_back(c):
                xc = x1c[c % 2]
                for j in range(4):
                    tt = c * 4 + j
                    js = slice(j * 128, (j + 1) * 128)
                    pi = tt % 2
                    bks = [pi * 4 + i for i in range(4)]

                    def fn(e, js=js, pi=pi):
                        ins = None
                        for k in range(4):
                            for dc in range(4):
                                ins = e.matmul(bank(pi * 4 + dc), lhsT=omT[:, k, js],
                                               rhs=wo[:, k, dc * 512:(dc + 1) * 512], start=(k == 0), stop=(k == 3))
                        return ins
                    P.emit("pe", fn, reads=["omT%d" % h for h in range(4)] + ["wom"], writes=[pk(b) for b in bks])
                    xk = "x1c%d_%d" % (c % 2, j)
                    for dc in range(4):
                        dsl = slice(dc * 512, (dc + 1) * 512)
                        P.emit("dve", lambda e, xc=xc, j=j, pi=pi, dc=dc, dsl=dsl: e.scalar_tensor_tensor(
                            out=xc[:, j, dsl], in0=xc[:, j, dsl], scalar=ALPHA, in1=bank(pi * 4 + dc), op0=ALU.mult, op1=ALU.add),
                            reads=[xk, pk(pi * 4 + dc)], writes=[xk])
                    layernorm(xc[:, j, :], xk, gt, bt, "lD", *lns)
                    P.dma("sp", X2[tt * 128:(tt + 1) * 128, :], xc[:, j, :], reads=[xk], src="d_so%d_%d" % (c % 2, j))
            d_front(0)
            for c in range(4):
                if c + 1 < 4:
                    load_x1(c + 1)
                d_mid(c)
                if c + 1 < 4:
                    d_front(c + 1)
                d_back(c)
            P.fence()
            P.build()

        es_bd.close()

        if LAST_PHASE >= "E":
          with ExitStack() as es:
            sb = lambda n, s, d: es.enter_context(nc.sbuf_tensor("sb_" + n, s, d))
            hT = sb("hT", [128, 64, 512], BF16)
            x2T = [sb("x2T0", [128, 16, 512], BF16)] * 2
            xs = sb("xsE", [128, D], F32)
            xb = sb("xbE", [128, D], BF16)
            NWB = 3
            w1b = [sb("w1b%d" % i, [128, 16, 256], BF16) for i in range(NWB)]
            w2b = [sb("w2b%d" % i, [128, 4, 1024], BF16) for i in range(NWB)]
            gt = sb("gtE", [128, D], F32); bt = sb("btE", [128, D], F32)
            xr = [sb("xrE%d" % i, [128, 1024], F32) for i in range(2)]
            xp4 = sb("xp4", [128, 4, D], F32)
            rbf = [sb("rbf%d" % i, [128, 512], BF16) for i in range(4)]
            lns = ln_scratch(sb, "E")
            P.dma("sp", gt[:], ln3_g.partition_broadcast(128), writes=["lng"], src="d_g")
            P.dma("sp", bt[:], ln3_b.partition_broadcast(128), writes=["lnb"], src="d_b")
            w1src = w_mlp_in.rearrange("(k p) c -> p k c", p=128)
            w2src = w_mlp_out.rearrange("(f p) c -> p f c", p=128)
            w1c = [0]
            w2c = [0]
            rc = [0]
            xrc = [0]

            def make_x2T(c):
                for j in range(4):
                    tt = c * 4 + j
                    js = slice(j * 128, (j + 1) * 128)
                    P.dma("sp", xs[:], X2[tt * 128:(tt + 1) * 128, :], writes=["xsE"], src="d_x0")
                    P.emit("dve", lambda e: e.tensor_copy(out=xb[:], in_=xs[:]), reads=["xsE"], writes=["xb"])
                    for half in range(2):
                        transpose_tiles([xb[:, (half * 8 + i) * 128:(half * 8 + i + 1) * 128] for i in range(8)], ["xb"],
                                        [(x2T[c % 2][:, half * 8:half * 8 + 8, js], "x2T0", 0, 8, 128, "act" if half else "dve")])

            make_x2T(0)
            for c in range(4):
                xT_ = x2T[c % 2]
                xTk = "x2T0"
                for fg in range(32):
                    wi = w1c[0] % NWB
                    w1c[0] += 1
                    P.dma("pool", w1b[wi][:], W1s[fg], writes=["w1b%d" % wi], src="d_w1%d" % wi)
                    for fi in range(2):
                        ft = fg * 2 + fi
                        b = next_bank()

                        def fn(e, wi=wi, fi=fi, b=b, xT_=xT_):
                            ins = None
                            for k in range(16):
                                ins = e.matmul(bank(b), lhsT=w1b[wi][:, k, fi * 128:(fi + 1) * 128], rhs=xT_[:, k, :], start=(k == 0), stop=(k == 15))
                            return ins
                        P.emit("pe", fn, reads=["w1b%d" % wi, xTk], writes=[pk(b)])
                        ri = rc[0] % 4
                        rc[0] += 1
                        P.emit("act", lambda e, ri=ri, b=b: e.activation(out=rbf[ri][:], in_=bank(b), func=AF.Relu), reads=[pk(b)], writes=["rbf%d" % ri])
                        P.emit("dve", lambda e, ri=ri, ft=ft: e.tensor_tensor(out=hT[:, ft, :], in0=rbf[ri][:], in1=rbf[ri][:], op=ALU.mult),
                               reads=["rbf%d" % ri], writes=["hT%d" % (ft // 4)])
                if c + 1 < 4:
                    make_x2T(c + 1)
                for dh in range(2):
                    for fq in range(16):
                        wi = w2c[0] % NWB
                        w2c[0] += 1
                        P.dma("pool", w2b[wi][:], W2s[dh, fq], writes=["w2b%d" % wi], src="d_w2%d" % wi)

                        def fn(e, wi=wi, fq=fq):
                            ins = None
                            for fi in range(4):
                                ft = fq * 4 + fi
                                for j in range(4):
                                    for dcl in range(2):
                                        ins = e.matmul(bank(j * 2 + dcl), lhsT=hT[:, ft, j * 128:(j + 1) * 128],
                                                       rhs=w2b[wi][:, fi, dcl * 512:(dcl + 1) * 512], start=(ft == 0), stop=(ft == 63))
                            return ins
                        P.emit("pe", fn, reads=["w2b%d" % wi, "hT%d" % fq], writes=[pk(b) for b in range(8)])
                    for j in range(4):
                        tt = c * 4 + j
                        xi = xrc[0] % 2
                        xrc[0] += 1
                        P.dma("sp", xr[xi][:], X2[tt * 128:(tt + 1) * 128, dh * 1024:(dh + 1) * 1024], writes=["xr%d" % xi], src="d_xr%d" % xi)
                        for dcl in range(2):
                            P.emit("dve", lambda e, j=j, xi=xi, dh=dh, dcl=dcl: e.scalar_tensor_tensor(
                                out=xp4[:, j, dh * 1024 + dcl * 512:dh * 1024 + (dcl + 1) * 512], in0=xr[xi][:, dcl * 512:(dcl + 1) * 512],
                                scalar=ALPHA, in1=bank(j * 2 + dcl), op0=ALU.mult, op1=ALU.add),
                                reads=["xr%d" % xi, pk(j * 2 + dcl)], writes=["xp4_%d" % j])
                for j in range(4):
                    tt = c * 4 + j
                    layernorm(xp4[:, j, :], "xp4_%d" % j, gt, bt, "lE", *lns, gb_eng="dve")
                    P.dma("sp", out[tt * 128:(tt + 1) * 128, :], xp4[:, j, :], reads=["xp4_%d" % j], src="d_out%d" % j)
            P.fence()
            P.build()
    return nc


def _rope_tables():
    f32 = np.float32
    t = np.arange(S)
    row = (t // 64).astype(f32)
    col = (t % 64).astype(f32)

    def ang(rot_dim):
        quarter = rot_dim // 4
        inv = (f32(10000.0) ** (-np.arange(quarter, dtype=f32) / f32(quarter))).astype(f32)
        return np.concatenate([row[:, None] * inv[None, :], col[:, None] * inv[None, :]], axis=-1).astype(f32)
    a, b = ang(128), ang(64)
    return (np.cos(a).astype(f32), np.sin(a).astype(f32), np.cos(b).astype(f32), np.sin(b).astype(f32))


_NC_CACHE = {}


def kernel(**inputs):
    f32 = np.float32
    if "nc" not in _NC_CACHE:
        _NC_CACHE["nc"] = build_program()
    nc = _NC_CACHE["nc"]
    cA, sA, cB, sB = _rope_tables()
    shared = {}
    for k in ("w_in", "w_uq", "w_ukv", "w_out", "w_q_mem", "w_kv_mem", "w_o_mem", "w_mlp_in", "w_mlp_out"):
        shared[k] = np.ascontiguousarray(np.asarray(inputs[k], dtype=f32)[0])
    for k in ("g_qa", "g_ka", "g_cq", "g_ckv", "ln1_g", "ln1_b", "mem_ln_g", "mem_ln_b", "ln2_g", "ln2_b", "ln3_g", "ln3_b"):
        shared[k] = np.ascontiguousarray(np.asarray(inputs[k], dtype=f32)[0][None, :])
    shared.update(cosA=cA, sinA=sA, cosB=cB, sinB=sB, ident=np.eye(128, dtype=f32))
    xs = np.asarray(inputs["x"], dtype=f32)
    ms = np.asarray(inputs["mem"], dtype=f32)
    in_maps = []
    for b in range(NCORES):
        m = dict(shared)
        m["x"] = np.ascontiguousarray(xs[b])
        m["mem"] = np.ascontiguousarray(ms[b])
        in_maps.append(m)
    ncores = getattr(kernel, "debug_cores", NCORES)
    in_maps = in_maps[:ncores]
    res = run_bass_kernel_spmd(nc, in_maps, core_ids=list(range(ncores)))
    if DEBUG_OUT:
        kernel.last = res.results
    outs = [np.asarray(r["out"], dtype=f32) for r in res.results]
    return np.stack(outs, axis=0)
```
